# Optimizing a Trainium2 kernel written in Bass

```python
import math
import jax
import jax.numpy as jnp
from jax import lax
import numpy as np

D_MODEL = 2048
BATCH = 4
SEQ = 2048
DEPTH = 1
DEC_BATCH = 128
DEC_SEQ = 8
PAST_LEN = 8192
PAGE_SIZE = 128

HEAD_DIM = 128
MIX_WIDTH = D_MODEL
SWA_Q_HEADS = MIX_WIDTH // 2 // HEAD_DIM
SWA_KV_HEADS = 2
SWA_GROUP = SWA_Q_HEADS // SWA_KV_HEADS
SWA_WIDTH = SWA_Q_HEADS * HEAD_DIM
SWA_KV_WIDTH = SWA_KV_HEADS * HEAD_DIM
X_HEADS = 4
X_WIDTH = X_HEADS * HEAD_DIM
CONV_WIDTH = MIX_WIDTH - SWA_WIDTH - X_WIDTH
CONV_K = 3
WINDOW = 128
ATTN_BLOCK = WINDOW
NUM_BUCKETS = 32
MAX_DISTANCE = WINDOW
MEM_TOKENS = 256
D_FF = 4 * D_MODEL
EPS = 1e-6
NEG = -1e30
IN_SPLITS = (CONV_WIDTH, 2 * CONV_WIDTH, 3 * CONV_WIDTH, 3 * CONV_WIDTH + SWA_WIDTH,
             3 * CONV_WIDTH + SWA_WIDTH + SWA_KV_WIDTH, 3 * CONV_WIDTH + SWA_WIDTH + 2 * SWA_KV_WIDTH)
IN_WIDTH = IN_SPLITS[-1] + X_WIDTH

kernel_name = 'hybrid_conv_swa_memory_decode_step'


def rmsnorm(x, g):
    xf = x.astype(jnp.float32)
    y = xf * lax.rsqrt(jnp.mean(xf * xf, axis=-1, keepdims=True) + EPS)
    return (y * g.astype(jnp.float32)).astype(x.dtype)


def rel_bucket(dist):
    n = jnp.maximum(dist, 0)
    max_exact = NUM_BUCKETS // 2
    nf = jnp.maximum(n, 1).astype(jnp.float32)
    large = max_exact + (jnp.log(nf / max_exact) / math.log(MAX_DISTANCE / max_exact)
                         * (NUM_BUCKETS - max_exact)).astype(jnp.int32)
    large = jnp.minimum(large, NUM_BUCKETS - 1)
    return jnp.where(n < max_exact, n, large)


def rel_bias(dist, table):
    b = jnp.moveaxis(table[rel_bucket(dist)], -1, 0)
    return b.reshape(SWA_KV_HEADS, SWA_GROUP, dist.shape[0], dist.shape[1])


def sink_attention(q, k, v, bias, valid, sink):
    s = jnp.einsum('...qhgd,...khd->...hgqk', q, k).astype(jnp.float32) * (HEAD_DIM ** -0.5)
    s = jnp.where(valid, s + bias.astype(jnp.float32), NEG)
    sink_l = sink.astype(jnp.float32).reshape(SWA_KV_HEADS, SWA_GROUP, 1, 1)
    m = jnp.maximum(jnp.max(s, axis=-1, keepdims=True), sink_l)
    p = jnp.exp(s - m)
    w = p / (jnp.sum(p, axis=-1, keepdims=True) + jnp.exp(sink_l - m))
    return jnp.einsum('...hgqk,...khd->...qhgd', w.astype(v.dtype), v)


def mixer_projections(x, g_mix, w_in, g_q_swa, g_k_swa, g_q_x):
    bsz, length = x.shape[0], x.shape[1]
    z = rmsnorm(x, g_mix) @ w_in
    b, c, hc, q, k, v, qx = jnp.split(z, IN_SPLITS, axis=-1)
    q = rmsnorm(q.reshape(bsz, length, SWA_Q_HEADS, HEAD_DIM), g_q_swa)
    k = rmsnorm(k.reshape(bsz, length, SWA_KV_HEADS, HEAD_DIM), g_k_swa)
    v = v.reshape(bsz, length, SWA_KV_HEADS, HEAD_DIM)
    qx = rmsnorm(qx.reshape(bsz, length, X_HEADS, HEAD_DIM), g_q_x)
    return b, c * hc, q, k, v, qx


def short_conv(u_ext, w):
    length = u_ext.shape[1] - (CONV_K - 1)
    out = w[0] * u_ext[:, 0:length]
    for i in range(1, CONV_K):
        out = out + w[i] * u_ext[:, i:i + length]
    return out


def swa_prompt(q, k, v, sink, table):
    bsz, seq = q.shape[0], q.shape[1]
    nb = seq // ATTN_BLOCK
    qb = q.reshape(bsz, nb, ATTN_BLOCK, SWA_KV_HEADS, SWA_GROUP, HEAD_DIM)

    def with_prev(t):
        tb = t.reshape(bsz, nb, ATTN_BLOCK, SWA_KV_HEADS, HEAD_DIM)
        prev = jnp.pad(tb[:, :-1], ((0, 0), (1, 0), (0, 0), (0, 0), (0, 0)))
        return jnp.concatenate([prev, tb], axis=2)

    qi = jnp.arange(ATTN_BLOCK)[:, None]
    ki = jnp.arange(2 * ATTN_BLOCK)[None, :]
    dist = ATTN_BLOCK + qi - ki
    band = (dist >= 0) & (dist < WINDOW)
    exists = (jnp.arange(nb)[:, None, None] > 0) | (ki[None] >= ATTN_BLOCK)
    valid = (band[None] & exists)[:, None, None]
    o = sink_attention(qb, with_prev(k), with_prev(v), rel_bias(dist, table), valid, sink)
    return o.reshape(bsz, seq, SWA_WIDTH)


def swa_sample(q, k_new, v_new, k_buf, v_buf, sink, table):
    bsz, t_len = q.shape[0], q.shape[1]
    k_all = jnp.concatenate([k_buf, k_new], axis=1)
    v_all = jnp.concatenate([v_buf, v_new], axis=1)
    dist = jnp.arange(t_len)[:, None] + WINDOW - jnp.arange(WINDOW + t_len)[None, :]
    valid = (dist >= 0) & (dist < WINDOW)
    qg = q.reshape(bsz, t_len, SWA_KV_HEADS, SWA_GROUP, HEAD_DIM)
    o = sink_attention(qg, k_all, v_all, rel_bias(dist, table), valid, sink)
    return o.reshape(bsz, t_len, SWA_WIDTH), k_all[:, -WINDOW:], v_all[:, -WINDOW:]


def memory_kv(mem, g_mem, w_mem_k, w_mem_v, g_k_x):
    bsz, m = mem.shape[0], mem.shape[1]
    h = rmsnorm(mem, g_mem)
    mk = rmsnorm((h @ w_mem_k).reshape(bsz, m, X_HEADS, HEAD_DIM), g_k_x)
    mv = (h @ w_mem_v).reshape(bsz, m, X_HEADS, HEAD_DIM)
    return mk, mv


def cross_attention(qx, mk, mv):
    s = jnp.einsum('bqhd,bkhd->bhqk', qx, mk).astype(jnp.float32) * (HEAD_DIM ** -0.5)
    w = jax.nn.softmax(s, axis=-1)
    o = jnp.einsum('bhqk,bkhd->bqhd', w.astype(mv.dtype), mv)
    return o.reshape(qx.shape[0], qx.shape[1], X_WIDTH)


def finish(x, y_conv, y_swa, y_x, w_out, g_mlp, w_up, w_down):
    x = x + jnp.concatenate([y_conv, y_swa, y_x], axis=-1) @ w_out
    a = jnp.square(jax.nn.relu(rmsnorm(x, g_mlp) @ w_up))
    return x + a @ w_down


def setup_inputs(seed: int = 0) -> dict:
    key = jax.random.key(seed)
    ks = jax.random.split(key, 32)

    def nrm(k, shape, scale=1.0):
        return scale * jax.random.normal(k, shape, jnp.float32)

    def gain(k, shape):
        return 1.0 + 0.01 * jax.random.normal(k, shape, jnp.float32)

    return {
        'x_prompt': nrm(ks[0], (BATCH, SEQ, D_MODEL)),
        'x_sample': nrm(ks[1], (DEC_BATCH, DEC_SEQ, D_MODEL)),
        'mem_prompt': nrm(ks[2], (BATCH, MEM_TOKENS, D_MODEL)),
        'cache_conv': nrm(ks[3], (DEPTH, DEC_BATCH, CONV_K - 1, CONV_WIDTH)),
        'cache_swa_k': nrm(ks[4], (DEPTH, DEC_BATCH, WINDOW, SWA_KV_HEADS, HEAD_DIM)),
        'cache_swa_v': nrm(ks[5], (DEPTH, DEC_BATCH, WINDOW, SWA_KV_HEADS, HEAD_DIM)),
        'cache_mem_k': nrm(ks[6], (DEPTH, DEC_BATCH, MEM_TOKENS, X_HEADS, HEAD_DIM)),
        'cache_mem_v': nrm(ks[7], (DEPTH, DEC_BATCH, MEM_TOKENS, X_HEADS, HEAD_DIM)),
        'rel_bias_table': nrm(ks[8], (NUM_BUCKETS, SWA_Q_HEADS), 0.5),
        'g_mix': gain(ks[9], (DEPTH, D_MODEL)),
        'w_in': nrm(ks[10], (DEPTH, D_MODEL, IN_WIDTH), D_MODEL ** -0.5),
        'conv_w': nrm(ks[11], (DEPTH, CONV_K, CONV_WIDTH), CONV_K ** -0.5),
        'g_q_swa': gain(ks[12], (DEPTH, HEAD_DIM)),
        'g_k_swa': gain(ks[13], (DEPTH, HEAD_DIM)),
        'sinks': nrm(ks[14], (DEPTH, SWA_Q_HEADS), 0.5),
        'g_q_x': gain(ks[15], (DEPTH, HEAD_DIM)),
        'g_k_x': gain(ks[16], (DEPTH, HEAD_DIM)),
        'g_mem': gain(ks[17], (DEPTH, D_MODEL)),
        'w_mem_k': nrm(ks[18], (DEPTH, D_MODEL, X_WIDTH), D_MODEL ** -0.5),
        'w_mem_v': nrm(ks[19], (DEPTH, D_MODEL, X_WIDTH), D_MODEL ** -0.5),
        'w_out': nrm(ks[20], (DEPTH, MIX_WIDTH, D_MODEL), MIX_WIDTH ** -0.5),
        'g_mlp': gain(ks[21], (DEPTH, D_MODEL)),
        'w_up': nrm(ks[22], (DEPTH, D_MODEL, D_FF), D_MODEL ** -0.5),
        'w_down': nrm(ks[23], (DEPTH, D_FF, D_MODEL), D_FF ** -0.5),
    }


def reference(x_prompt, x_sample, mem_prompt, cache_conv, cache_swa_k, cache_swa_v, cache_mem_k, cache_mem_v,
              rel_bias_table, g_mix, w_in, conv_w, g_q_swa, g_k_swa, sinks, g_q_x, g_k_x, g_mem,
              w_mem_k, w_mem_v, w_out, g_mlp, w_up, w_down):
    xp, xs = x_prompt, x_sample
    p_conv, p_k, p_v, p_mk, p_mv = [], [], [], [], []
    s_conv, s_k, s_v = [], [], []
    for l in range(DEPTH):
        b, u, q, k, v, qx = mixer_projections(xp, g_mix[l], w_in[l], g_q_swa[l], g_k_swa[l], g_q_x[l])
        u_ext = jnp.pad(u, ((0, 0), (CONV_K - 1, 0), (0, 0)))
        y_conv = b * short_conv(u_ext, conv_w[l])
        y_swa = swa_prompt(q, k, v, sinks[l], rel_bias_table)
        mk, mv = memory_kv(mem_prompt, g_mem[l], w_mem_k[l], w_mem_v[l], g_k_x[l])
        y_x = cross_attention(qx, mk, mv)
        p_conv.append(u_ext[:, -(CONV_K - 1):])
        p_k.append(k[:, -WINDOW:])
        p_v.append(v[:, -WINDOW:])
        p_mk.append(mk)
        p_mv.append(mv)
        xp = finish(xp, y_conv, y_swa, y_x, w_out[l], g_mlp[l], w_up[l], w_down[l])

        b, u, q, k, v, qx = mixer_projections(xs, g_mix[l], w_in[l], g_q_swa[l], g_k_swa[l], g_q_x[l])
        u_ext = jnp.concatenate([cache_conv[l], u], axis=1)
        y_conv = b * short_conv(u_ext, conv_w[l])
        y_swa, k_buf, v_buf = swa_sample(q, k, v, cache_swa_k[l], cache_swa_v[l], sinks[l], rel_bias_table)
        y_x = cross_attention(qx, cache_mem_k[l], cache_mem_v[l])
        s_conv.append(u_ext[:, -(CONV_K - 1):])
        s_k.append(k_buf)
        s_v.append(v_buf)
        xs = finish(xs, y_conv, y_swa, y_x, w_out[l], g_mlp[l], w_up[l], w_down[l])

    return (xp, xs, jnp.stack(p_conv), jnp.stack(p_k), jnp.stack(p_v), jnp.stack(p_mk), jnp.stack(p_mv),
            jnp.stack(s_conv), jnp.stack(s_k), jnp.stack(s_v))
```

```python
import math
from contextlib import ExitStack

import numpy as np
import concourse.bass as bass
import concourse.mybir as mybir
from concourse.bass_utils import run_bass_kernel_spmd

F32 = mybir.dt.float32
BF16 = mybir.dt.bfloat16
AF = mybir.ActivationFunctionType
ALU = mybir.AluOpType

NCORES = 8
D = 2048
TP = 1024
TS = 128
T = TP + TS
XC = T + 2
EPS = 1e-6
SCALE = 128 ** -0.5
ENGS = ("pe", "act", "dve", "pool", "sp")
STOP = 99
SKIP = 0


class _Stop(Exception):
    pass


class Sem:
    def __init__(self, h, step):
        self.h = h
        self.n = 0
        self.step = step


class Res:
    def __init__(self, name="", excl=False, seed=None, multi=False):
        self.name = name
        self.w = None
        self.r = dict(seed) if seed else {}
        self.multi = multi
        self.wset = {}
        self.excl = excl


class Prog:
    def __init__(self, nc, stack):
        self.nc = nc
        self.stack = stack
        self.q = {e: [] for e in ENGS}
        self.waited = {e: {} for e in ENGS}
        self.esem = {e: self.new_sem("prog_" + e, 1) for e in ("pe", "act", "dve", "pool")}
        self.dma_sems = []
        self.pending = {e: [] for e in ENGS}
        self.snap_exclude = set()

    def new_sem(self, name, step):
        h = self.stack.enter_context(self.nc.semaphore(name))
        return Sem(h, step)

    def dma_sem(self, name):
        s = self.new_sem(name, 16)
        self.dma_sems.append(s)
        return s

    def emit(self, eng, fn, deps=(), sem=None, inc=True):
        waits = []
        alld = list(deps) + self.pending[eng]
        self.pending[eng] = []
        for d in alld:
            if d is None:
                continue
            s, v = d
            if self.waited[eng].get(id(s), 0) >= v:
                continue
            self.waited[eng][id(s)] = v
            waits.append((s, v))
        tok = None
        if inc:
            if sem is None:
                sem = self.esem[eng]
            sem.n += sem.step
            tok = (sem, sem.n)
            self.q[eng].append((waits, fn, sem))
        else:
            self.q[eng].append((waits, fn, None))
        return tok

    @staticmethod
    def deps(reads, writes):
        d = []
        for r in reads:
            if r.multi:
                d.extend(r.wset.values())
            else:
                d.append(r.w)
        for r in writes:
            if not r.multi:
                d.append(r.w)
            d.extend(r.r.values())
        return d

    @staticmethod
    def use(tok, reads, writes):
        s, v = tok
        for r in reads:
            o = r.r.get(id(s))
            if o is None or o[1] < v:
                r.r[id(s)] = tok
        for r in writes:
            if r.multi:
                o = r.wset.get(id(s))
                if o is None or o[1] < v:
                    r.wset[id(s)] = tok
            else:
                r.w = tok
                r.r = {}

    @staticmethod
    def _split(reads, writes):
        ex = tuple(r for r in reads if r.excl)
        if ex:
            reads = tuple(r for r in reads if not r.excl)
            writes = tuple(writes) + ex
        return reads, writes

    def op(self, eng, fn, reads=(), writes=(), sem=None):
        reads, writes = self._split(reads, writes)
        tok = self.emit(eng, fn, self.deps(reads, writes), sem)
        self.use(tok, reads, writes)
        return tok

    def group(self, eng, fns, reads=(), writes=()):
        reads, writes = self._split(reads, writes)
        d = self.deps(reads, writes)
        n = len(fns)
        tok = None
        for i, fn in enumerate(fns):
            tok = self.emit(eng, fn, d if i == 0 else (), inc=(i == n - 1))
        self.use(tok, reads, writes)
        return tok

    def snapshot(self):
        snap = {}
        for sm in list(self.esem.values()) + self.dma_sems:
            if sm.n > 0 and id(sm) not in self.snap_exclude:
                snap[id(sm)] = (sm, sm.n)
        return snap

    def barrier(self):
        toks = [(s, s.n) for s in self.esem.values() if s.n > 0]
        toks += [(s, s.n) for s in self.dma_sems if s.n > 0]
        for e in ENGS:
            self.pending[e] = self.pending[e] + toks

    def replay(self, block):
        def run(name, e):
            for waits, fn, sem in self.q[name]:
                for s, v in waits:
                    e.wait_ge(s.h, v)
                ins = fn(e)
                if sem is not None:
                    ins.then_inc(sem.h, sem.step)

        @block.tensor
        def _(e):
            run("pe", e)

        @block.scalar
        def _(e):
            run("act", e)

        @block.vector
        def _(e):
            run("dve", e)

        @block.gpsimd
        def _(e):
            run("pool", e)

        @block.sync
        def _(e):
            run("sp", e)


def MM(out, lhsT, rhs, start, stop):
    return lambda e: e.matmul(out, lhsT=lhsT, rhs=rhs, start=start, stop=stop)


def TR(out, in_, ident):
    return lambda e: e.transpose(out, in_, ident)


def ACTF(out, in_, func, scale=None, bias=None, accum_out=None):
    kw = {}
    if scale is not None:
        kw["scale"] = scale
    if bias is not None:
        kw["bias"] = bias
    if accum_out is not None:
        kw["accum_out"] = accum_out
    return lambda e: e.activation(out=out, in_=in_, func=func, **kw)


def TT(out, in0, in1, op):
    return lambda e: e.tensor_tensor(out=out, in0=in0, in1=in1, op=op)


def TSC(out, in0, s1, op0, s2=None, op1=None):
    if op1 is None:
        return lambda e: e.tensor_scalar(out=out, in0=in0, scalar1=s1, scalar2=None, op0=op0)
    return lambda e: e.tensor_scalar(out=out, in0=in0, scalar1=s1, scalar2=s2, op0=op0, op1=op1)


def STT(out, in0, scalar, in1, op0, op1):
    return lambda e: e.scalar_tensor_tensor(out=out, in0=in0, scalar=scalar, in1=in1, op0=op0, op1=op1)


def CP(out, in_):
    return lambda e: e.tensor_copy(out=out, in_=in_)


def RCP(out, in_):
    return lambda e: e.reciprocal(out=out, in_=in_)


def RCPF(out, in_):
    return lambda e: e.reciprocal_approx_fast(out=out, in_=in_)


def MSET(ap, val):
    return lambda e: e.memset(ap, val)


def DMA(out, in_, **kw):
    kw.setdefault("allow_slow_non_contiguous", True)
    return lambda e: e.dma_start(out=out, in_=in_, **kw)


class Arena:
    def __init__(self, nc, stack, nbytes):
        self.nbytes = nbytes
        self.t32 = stack.enter_context(nc.sbuf_tensor("arena", [128, nbytes // 4], F32))
        self.t16 = self.t32.bitcast(BF16)
        self.top = 0
        self.marks = []

    def alloc(self, dt, shape, np_=128):
        sz = 4 if dt is F32 else 2
        n = 1
        for s in shape:
            n *= s
        nb = (n * sz + 63) // 64 * 64
        off = self.top
        self.top += nb
        assert self.top <= self.nbytes, f"arena overflow {self.top} > {self.nbytes}"
        return self.view(off, dt, shape, np_)

    def view(self, off, dt, shape, np_=128):
        sz = 4 if dt is F32 else 2
        t = self.t32 if dt is F32 else self.t16
        pstep = self.nbytes // sz
        dims = []
        st = 1
        for s in reversed(shape):
            dims.append([st, s])
            st *= s
        dims.reverse()
        return bass.AP(t, off // sz, [[pstep, np_]] + dims)

    def mark(self):
        self.marks.append(self.top)

    def release(self):
        self.top = self.marks.pop()


def ap_of(ap, off_elems, dims, np_=None):
    p = ap.ap[0]
    return bass.AP(ap.tensor, ap.offset + off_elems, [[p[0], p[1] if np_ is None else np_]] + [list(d) for d in dims])


def build():
    nc = bass.Bass("TRN2", target_bir_lowering=False)

    def din(name, shape):
        return nc.dram_tensor(name, list(shape), F32, kind="ExternalInput").ap()

    def dout(name, shape):
        return nc.dram_tensor(name, list(shape), F32, kind="ExternalOutput").ap()

    xp = din("xp", [TP, D]); xh = din("xh", [128, D]); xs = din("xs", [TS, D]); xm = din("xm", [256, D])
    cconv = din("cconv", [32, 512])
    ck = din("ck", [16, 128, 256]); cv = din("cv", [16, 128, 256])
    cmk = din("cmk", [16, 256, 512]); cmv = din("cmv", [16, 256, 512])
    hflag = din("hflag", [128, 1]); oh1 = din("oh1", [32, 128]); bd = din("bd", [128, 128]); identd = din("ident", [128, 128]); aidentd = din("aident", [128, 128])
    w_in = din("w_in", [D, 3584]); w_mk = din("w_mk", [D, 512]); w_mv = din("w_mv", [D, 512])
    w_out = din("w_out", [D, D]); w_up = din("w_up", [D, 8192]); w_down = din("w_down", [8192, D])
    table = din("table", [32, 8]); g_mix = din("g_mix", [1, D]); conv_w = din("conv_w", [3, 512])
    g_q = din("g_q", [1, 128]); g_k = din("g_k", [1, 128]); sinks = din("sinks", [1, 8])
    g_qx = din("g_qx", [1, 128]); g_kx = din("g_kx", [1, 128]); g_mem = din("g_mem", [1, D]); g_mlp = din("g_mlp", [1, D])

    yp = dout("yp", [TP, D]); ys = dout("ys", [TS, D])
    o_pconv = dout("o_pconv", [2, 512]); o_pk = dout("o_pk", [128, 256]); o_pv = dout("o_pv", [128, 256])
    o_mk = dout("o_mk", [256, 512]); o_mv = dout("o_mv", [256, 512])
    o_sconv = dout("o_sconv", [16, 2, 512]); o_sk = dout("o_sk", [16, 128, 256]); o_sv = dout("o_sv", [16, 128, 256])
    fscr = nc.dram_tensor("fscr", [8, 384], F32, kind="Internal").ap()

    stack = ExitStack()
    try:
      with stack:
        P = Prog(nc, stack)

        def finalize():
            P.barrier()
            P.emit("sp", lambda e: e.nop(), inc=False)
            P.emit("act", lambda e: e.nop(), inc=False)
            with nc.Block() as block:
                P.replay(block)

        def checkpoint(k):
            if STOP == k:
                finalize()
                raise _Stop()
        ARENA_BYTES = 207 * 1024
        ar = Arena(nc, stack, ARENA_BYTES)
        ps32_t = stack.enter_context(nc.psum_tensor("ps", [128, 4096], F32))
        ps16_t = ps32_t.bitcast(BF16)

        def bank32(b, n=512, np_=128):
            return bass.AP(ps32_t, b * 512, [[4096, np_], [1, n]])

        def bank16(b, n=1024):
            return bass.AP(ps16_t, b * 1024, [[8192, 128], [1, n]])

        bank_res = [Res(f"bank{i}", excl=True) for i in range(8)]
        bank_ctr = [0]

        bank_mod = [8]

        def nextbank():
            b = bank_ctr[0] % bank_mod[0]
            bank_ctr[0] += 1
            return b, bank_res[b]

        SLAB = [ar.alloc(BF16, [16, 512]) for _ in range(2)]
        slab_res = [Res("slab0"), Res("slab1")]
        slab_sem = [P.dma_sem("slab0"), P.dma_sem("slab1")]
        P.snap_exclude.update(id(x) for x in slab_sem)
        XT = ar.alloc(BF16, [16, XC]); xt_off = ar.top - ((16 * XC * 2 + 63) // 64 * 64)
        MIX = ar.alloc(BF16, [16, T]); mix_off = ar.top - 16 * T * 2
        IDF = ar.alloc(F32, [128]); IDB = ar.alloc(BF16, [128]); ONESB = ar.alloc(BF16, [128])
        R0 = ar.top
        xt_res = Res("XT")
        mix_res = Res("MIX")

        st_sem = P.dma_sem("st")
        semc = [0]

        def fresh_sem():
            semc[0] += 1
            return P.dma_sem(f"d{semc[0]}")

        def sp_load(out, in_, writes, reads=(), sem=None, **kw):
            return P.op("sp", DMA(out, in_, **kw), reads=reads, writes=writes, sem=sem or fresh_sem())

        def sp_store(out, in_, reads, sem=None, **kw):
            return P.op("sp", DMA(out, in_, **kw), reads=reads, writes=(), sem=sem or st_sem)

        def prow(ap, p0, np_, c0, n):
            return bass.AP(ap.tensor, ap.offset + p0 * ap.ap[0][0] + c0, [[ap.ap[0][0], np_], [1, n]])

        slab_list = []

        def add_slab(wap, r0, c0):
            slab_list.append((wap, r0, c0))
            return len(slab_list) - 1

        slab_issued = [0]

        slab_qdone = [0]

        def issue_slab_quarter():
            i = slab_issued[0]
            if i >= len(slab_list):
                return
            wap, r0, c0 = slab_list[i]
            slot = i % 2
            qd = slab_qdone[0]
            src = wap[r0 + qd * 512: r0 + (qd + 1) * 512, c0:c0 + 512].rearrange("(k p) n -> p k n", p=128)
            dst = SLAB[slot][:, qd * 4:(qd + 1) * 4, :]
            P.op("pool", DMA(dst, src), writes=(slab_res[slot],), sem=slab_sem[slot])
            slab_qdone[0] += 1
            if slab_qdone[0] == 4:
                slab_qdone[0] = 0
                slab_issued[0] += 1

        def issue_slabs(upto):
            while slab_issued[0] <= min(upto, len(slab_list) - 1):
                issue_slab_quarter()

        S_MK = add_slab(w_mk, 0, 0)
        S_MV = add_slab(w_mv, 0, 0)
        S_KV = add_slab(w_in, 0, 2560)
        S_C = add_slab(w_in, 0, 512)
        S_H = add_slab(w_in, 0, 1024)
        S_B = add_slab(w_in, 0, 0)
        S_QX = add_slab(w_in, 0, 3072)
        S_Q0 = add_slab(w_in, 0, 1536)
        S_Q1 = add_slab(w_in, 0, 2048)
        S_OUT = [add_slab(w_out, 0, og * 512) for og in range(4)]
        S_MLP = []
        for j in range(4):
            ups = [add_slab(w_up, 0, (j * 4 + u) * 512) for u in range(4)]
            dns = [add_slab(w_down, j * 2048, og * 512) for og in range(4)]
            S_MLP.append((ups, dns))

        hold_after = [None]

        def use_slab(i):
            nxt = i + 1
            if hold_after[0] is not None:
                nxt = min(nxt, hold_after[0])
            issue_slabs(max(nxt, i))
            return SLAB[i % 2], slab_res[i % 2]

        if not (SKIP & 1):
            issue_slabs(0)

        GQ = ar.alloc(F32, [1]); GQX = ar.alloc(F32, [1]); HF = ar.alloc(F32, [1])
        GKB = ar.alloc(F32, [128]); GKXB = ar.alloc(F32, [128])
        CW = ar.alloc(F32, [3, 4])
        BDT = ar.alloc(F32, [128]); AIDF = ar.alloc(F32, [128])
        ECUR = ar.alloc(F32, [2, 4, 128]); EPREV = ar.alloc(F32, [2, 4, 128])
        EFIRST = ar.alloc(F32, [2, 4, 128]); ENEW = ar.alloc(F32, [2, 4, 128])
        SROW = ar.alloc(BF16, [2, 512])
        SMALL = ar.alloc(F32, [64])
        TBL = ar.alloc(F32, [8]); OH1 = ar.alloc(F32, [128]); FPAD = ar.alloc(F32, [384]); SK1 = ar.alloc(F32, [8]); SKE = ar.alloc(F32, [8])
        KT = ar.alloc(BF16, [2, 1280]); VT = ar.alloc(BF16, [10, 256])
        MKT = ar.alloc(BF16, [4, 256]); MV = ar.alloc(BF16, [2, 512])
        STAT = ar.alloc(F32, [32])
        RA = ar.top
        KNS = [ar.alloc(F32, [512]) for _ in range(2)]
        KNB = [ar.alloc(BF16, [512]) for _ in range(2)]
        JUNK128 = ar.alloc(BF16, [128])
        NXS = 4
        XST = [ar.alloc(F32, [D]) for _ in range(NXS)]
        XNB = [ar.alloc(BF16, [D]) for _ in range(2)]
        GB = ar.view(mix_off, F32, [D]); GMB = ar.view(mix_off + 8192, F32, [D])
        XM = ar.view(mix_off + 16384, BF16, [16, 256]); XH = ar.view(mix_off + 24576, BF16, [16, 128])
        JUNK = ar.view(mix_off + 28672, BF16, [D])
        xst_res = [Res() for _ in range(NXS)]; xnb_res = [Res(), Res()]; junk_res = Res(); stat_res = [Res(), Res()]
        xst_sem = [P.dma_sem(f"xst{i}") for i in range(NXS)]
        kns_sem = [P.dma_sem("kns_st0"), P.dma_sem("kns_st1")]
        msk = Res("masks")
        cst = Res("consts", multi=True)

        idr = Res("ident"); gbr = Res("gb"); gmbr = Res("gmb")
        sp_load(XST[0], xm[0:128, :], (xst_res[0],), sem=xst_sem[0])
        sp_load(IDF, identd, (idr, cst))
        sp_load(GMB, g_mem.partition_broadcast(128)[:, 0, :], (gmbr, cst))
        sp_load(XST[1], xm[128:256, :], (xst_res[1],), sem=xst_sem[1])
        sp_load(GB, g_mix.partition_broadcast(128)[:, 0, :], (gbr, cst))
        sp_load(XST[2], xh, (xst_res[2],), sem=xst_sem[2])
        sp_load(XST[3], xp[0:128, :], (xst_res[3],), sem=xst_sem[3])
        P.op("dve", CP(IDB, IDF), reads=(idr,), writes=(idr, cst))
        with nc.allow_non_contiguous_dma(reason="tiny constant loads"):
            sp_load(GQ, g_q.rearrange("o d -> d o"), (cst,))
            sp_load(GQX, g_qx.rearrange("o d -> d o"), (cst,))
            sp_load(HF, hflag, (cst,))
            sp_load(GKB, g_k.partition_broadcast(128)[:, 0, :], (cst,))
            sp_load(GKXB, g_kx.partition_broadcast(128)[:, 0, :], (cst,))
            sp_load(CW, conv_w.rearrange("i (c p) -> p i c", p=128), (cst,))
            sp_load(BDT, bd, (cst,))
            sp_load(AIDF, aidentd, (cst,))
            sp_load(ap_of(TBL, 0, [[1, 8]], 32), table, (cst,))
            sp_load(ap_of(OH1, 0, [[1, 128]], 32), oh1, (cst,))
            sp_load(SK1, sinks.partition_broadcast(128)[:, 0, :], (cst,))
        P.op("dve", MSET(ONESB, 1.0), writes=(cst,))
        def build_masks():
            P.op("dve", MSET(ap_of(FPAD, 0, [[1, 384]], 8), 0.0), writes=(msk,))
            if not (SKIP & 2):
                b, br = nextbank()
                P.group("pe", [MM(bank32(b, 128, 8), ap_of(TBL, 0, [[1, 8]], 32), ap_of(OH1, 0, [[1, 128]], 32), True, True)],
                        reads=(cst, msk), writes=(br,))
                P.op("act", ACTF(ap_of(FPAD, 127, [[1, 128]], 8), bank32(b, 128, 8), AF.Exp), reads=(br,), writes=(msk,))
                fs = Res("fscr")
                P.op("sp", DMA(fscr, ap_of(FPAD, 0, [[1, 384]], 8)), reads=(cst, msk), writes=(fs,), sem=fresh_sem())
                for h in range(2):
                    src_c = bass.AP(fscr.tensor, (4 * h) * 384 + 0, [[1, 128], [384, 4], [1, 128]])
                    src_p = bass.AP(fscr.tensor, (4 * h) * 384 + 128, [[1, 128], [384, 4], [1, 128]])
                    sp_load(ECUR[:, h, :, :], src_c, (msk,), reads=(fs,))
                    sp_load(EPREV[:, h, :, :], src_p, (msk,), reads=(fs,))

        def build_masks2():
            if not (SKIP & 2):
                for E in (ECUR, EPREV):
                    for h in range(2):
                        ev = E[:, h, :, :].rearrange("p g q -> p (g q)")
                        b, br = nextbank()
                        P.group("pe", [MM(bank32(b), AIDF, ev, True, True)], reads=(cst, msk), writes=(br,))
                        P.op("act", ACTF(ev, bank32(b), AF.Copy), reads=(br,), writes=(msk,))
            if not (SKIP & 8):
                P.op("dve", TSC(EFIRST.rearrange("p a g q -> p (a g q)"), EPREV.rearrange("p a g q -> p (a g q)"), HF[:, 0:1], ALU.mult),
                     reads=(cst, msk), writes=(msk,))
            if not (SKIP & 16):
                bd_b = ap_of(BDT, 0, [[0, 8], [1, 128]])
                P.op("dve", TT(ENEW.rearrange("p a g q -> p (a g) q"), ECUR.rearrange("p a g q -> p (a g) q"), bd_b, ALU.mult),
                     reads=(cst, msk), writes=(msk,))
            P.op("act", ACTF(SKE, SK1, AF.Exp), reads=(cst, msk), writes=(msk,))
            for hh in range(8):
                P.op("dve", TSC(SROW[:, hh // 4, (hh % 4) * 128:(hh % 4 + 1) * 128], ONESB, SKE[:, hh:hh + 1], ALU.mult),
                     reads=(cst, msk), writes=(msk,))


        checkpoint(0)

        evac_flip = [0]

        def evac_engine():
            evac_flip[0] ^= 1
            return "act" if evac_flip[0] else "dve"

        def copy_op(eng, out, in_):
            if eng == "act":
                return ACTF(out, in_, AF.Copy)
            return CP(out, in_)

        xm_res = Res("XM"); xh_res = Res("XH")
        tiles0 = [(xm[0:128, :], GMB, XM, xm_res, 0, gmbr), (xm[128:256, :], GMB, XM, xm_res, 128, gmbr),
                  (xh, GB, XH, xh_res, 0, gbr)]
        tiles0 += [(xp[i * 128:(i + 1) * 128, :], GB, XT, xt_res, 2 + i * 128, gbr) for i in range(8)]
        tiles0 += [(xs, GB, XT, xt_res, 2 + TP, gbr)]

        def norm_load(it):
            xs_ = it % NXS
            P.op("act", DMA(XST[xs_], tiles0[it][0]), writes=(xst_res[xs_],), sem=xst_sem[xs_])

        def norm_A(it):
            sl = it % 2
            xs_ = it % NXS
            gb = tiles0[it][1]
            ss = STAT[:, 2 * sl:2 * sl + 1]
            rs = STAT[:, 2 * sl + 1:2 * sl + 2]
            P.op("act", ACTF(JUNK, XST[xs_], AF.Square, accum_out=ss), reads=(xst_res[xs_],), writes=(junk_res, stat_res[sl]))
            P.op("act", ACTF(rs, ss, AF.Ln, scale=1.0 / D, bias=EPS), reads=(stat_res[sl],), writes=(stat_res[sl],))
            P.op("act", ACTF(rs, rs, AF.Exp, scale=-0.5), reads=(stat_res[sl],), writes=(stat_res[sl],))
            P.op("dve", STT(XNB[sl], XST[xs_], rs, gb, ALU.mult, ALU.mult), reads=(xst_res[xs_], stat_res[sl], tiles0[it][5]),
                 writes=(xnb_res[sl],))

        def norm_B(it):
            sl = it % 2
            _, _, dst, dst_res, dst_col0, _ = tiles0[it]
            for half in range(2):
                b, br = nextbank()
                fns = [TR(bank16(b)[:, j * 128:(j + 1) * 128], XNB[sl][:, (half * 8 + j) * 128:(half * 8 + j + 1) * 128], IDB)
                       for j in range(8)]
                P.group("pe", fns, reads=(xnb_res[sl], idr), writes=(br,))
                eng = evac_engine()
                P.op(eng, copy_op(eng, dst[:, half * 8:half * 8 + 8, dst_col0:dst_col0 + 128],
                                  bank16(b).rearrange("p (j t) -> p j t", j=8)),
                     reads=(br,), writes=(dst_res,))
            if it == 2:
                P.op("dve", CP(XT[:, :, 0:2], XH[:, :, 126:128]), reads=(xh_res,), writes=(xt_res,))

        checkpoint(1)
        kns_res = [Res(), Res()]; knb_res = [Res(), Res()]; hst_res = [Res(), Res()]
        mkt_res = Res("MKT"); mv_res = Res("MV")
        kt_res = Res("KT"); vt_res = Res("VT")

        def head_rms_tokmajor(bank_ap, nheads, gbc, out_f32, reads_bank, out_res, k):
            c0 = 8 + 4 * k
            for h in range(nheads):
                P.op("act", ACTF(JUNK128, bank_ap[:, h * 128:(h + 1) * 128], AF.Square,
                                 accum_out=STAT[:, c0 + h:c0 + h + 1]),
                     reads=(reads_bank,), writes=(junk_res, hst_res[k]))
            P.op("act", ACTF(STAT[:, c0:c0 + nheads], STAT[:, c0:c0 + nheads], AF.Ln, scale=1.0 / 128, bias=EPS),
                 reads=(hst_res[k],), writes=(hst_res[k],))
            P.op("act", ACTF(STAT[:, c0:c0 + nheads], STAT[:, c0:c0 + nheads], AF.Exp, scale=-0.5),
                 reads=(hst_res[k],), writes=(hst_res[k],))
            for h in range(nheads):
                P.op("dve", STT(out_f32[:, h * 128:(h + 1) * 128], bank_ap[:, h * 128:(h + 1) * 128],
                                STAT[:, c0 + h:c0 + h + 1], gbc, ALU.mult, ALU.mult),
                     reads=(reads_bank, hst_res[k], cst), writes=(out_res,))

        jobs = []
        jobs += [("mk", mt, S_MK) for mt in range(2)]
        jobs += [("mv", mt, S_MV) for mt in range(2)]
        jobs += [("kv", ti, S_KV) for ti in range(10)]

        def tk_mm(n):
            kind, idx, sidx = jobs[n]
            sl_ap, sl_res = use_slab(sidx)
            if kind in ("mk", "mv"):
                lsrc, lres, c0 = XM, xm_res, idx * 128
            elif idx == 0:
                lsrc, lres, c0 = XH, xh_res, 0
            else:
                lsrc, lres, c0 = XT, xt_res, 2 + (idx - 1) * 128
            b = 5 + n % 3
            br = bank_res[b]
            P.group("pe", [MM(bank32(b), lsrc[:, kc, c0:c0 + 128], sl_ap[:, kc, :], kc == 0, kc == 15)
                           for kc in range(16)], reads=(lres, sl_res), writes=(br,))
            return b, br

        def tk_post(n, st):
            kind, idx, _ = jobs[n]
            b, br = st
            k = n % 2
            KS, KB = KNS[k], KNB[k]
            if kind == "mk":
                head_rms_tokmajor(bank32(b), 4, GKXB, KS, br, kns_res[k], k)
                sp_store(o_mk[idx * 128:(idx + 1) * 128, :], KS, (kns_res[k],), sem=kns_sem[k])
                P.op("act", ACTF(KB, KS, AF.Copy), reads=(kns_res[k],), writes=(knb_res[k],))

                def part_b():
                    b2, br2 = nextbank()
                    P.group("pe", [TR(bank16(b2)[:, h * 128:(h + 1) * 128], KB[:, h * 128:(h + 1) * 128], IDB) for h in range(4)],
                            reads=(knb_res[k], cst), writes=(br2,))
                    P.op("dve", CP(MKT[:, :, idx * 128:(idx + 1) * 128], bank16(b2, 512).rearrange("p (h t) -> p h t", h=4)),
                         reads=(br2,), writes=(mkt_res,))
                return part_b
            elif kind == "mv":
                P.op("act", ACTF(KS, bank32(b), AF.Copy), reads=(br,), writes=(kns_res[k],))
                P.op("dve", CP(MV[:, idx, :], bank32(b)), reads=(br,), writes=(mv_res,))
                sp_store(o_mv[idx * 128:(idx + 1) * 128, :], KS, (kns_res[k],), sem=kns_sem[k])
                return None
            else:
                ti = idx
                head_rms_tokmajor(bank32(b), 2, GKB, KS, br, kns_res[k], k)
                P.op("act", ACTF(VT[:, ti, :], bank32(b)[:, 256:512], AF.Copy), reads=(br,), writes=(vt_res,))
                if ti >= 8:
                    P.op("act", ACTF(KS[:, 256:512], bank32(b)[:, 256:512], AF.Copy), reads=(br,), writes=(kns_res[k],))
                P.op("act", ACTF(KB[:, 0:256], KS[:, 0:256], AF.Copy), reads=(kns_res[k],), writes=(knb_res[k],))
                if ti == 8:
                    sp_store(o_pk, KS[:, 0:256], (kns_res[k],), sem=kns_sem[k])
                    sp_store(o_pv, KS[:, 256:512], (kns_res[k],), sem=kns_sem[k])
                if ti == 9:
                    for sq in range(16):
                        sp_store(o_sk[sq, 120:128, :], prow(KS, sq * 8, 8, 0, 256), (kns_res[k],), sem=kns_sem[k])
                        sp_store(o_sv[sq, 120:128, :], prow(KS, sq * 8, 8, 256, 256), (kns_res[k],), sem=kns_sem[k])
                def part_b():
                    b2, br2 = nextbank()
                    P.group("pe", [TR(bank16(b2)[:, h * 128:(h + 1) * 128], KB[:, h * 128:(h + 1) * 128], IDB) for h in range(2)],
                            reads=(knb_res[k], cst), writes=(br2,))
                    P.op("dve", CP(KT[:, :, ti * 128:(ti + 1) * 128], bank16(b2, 256).rearrange("p (h t) -> p h t", h=2)),
                         reads=(br2,), writes=(kt_res,))
                return part_b

        job_state = {"n": 0}
        pending_posts = []
        bank_mod[0] = 5

        def push_mm():
            n = job_state["n"]
            if n >= len(jobs):
                return
            pending_posts.append((n, tk_mm(n)))
            job_state["n"] = n + 1

        pend_b = [None]

        def pop_post():
            n, st = pending_posts.pop(0)
            bnew = tk_post(n, st)
            if pend_b[0] is not None:
                pend_b[0]()
            pend_b[0] = bnew

        NT0 = len(tiles0)
        norm_A(0)
        for it in range(NT0):
            if it + 1 < NT0:
                norm_A(it + 1)
            norm_B(it)
            if it + NXS < NT0:
                norm_load(it + NXS)
            if it >= 3:
                push_mm()
                if len(pending_posts) > 2:
                    pop_post()
        while job_state["n"] < len(jobs):
            push_mm()
            if len(pending_posts) > 2:
                pop_post()
        checkpoint(2)

        checkpoint(3)
        snap1 = P.snapshot()
        build_masks()
        mix_res.r.update(snap1)
        ar.top = RA
        QT = ar.alloc(BF16, [8, T]); QXT = ar.alloc(BF16, [4, T])
        RB = ar.top
        U = ar.alloc(F32, [4, 1026]); US = ar.alloc(F32, [4, 16, 10])

        TGH = [(0, 386), (386, 384), (770, 384)]
        TGM = [(2, 384), (386, 384), (770, 384)]

        def projB(slab_idx, groups, evac, hook=None):
            sl_ap, sl_res = use_slab(slab_idx)
            pend = None
            for c in range(4):
                if hook is not None and c > 0:
                    hook()
                for gi, (c0, n) in enumerate(groups):
                    b, br = nextbank()
                    P.group("pe", [MM(bank32(b, n), sl_ap[:, kc, c * 128:(c + 1) * 128], XT[:, kc, c0:c0 + n], kc == 0, kc == 15)
                                   for kc in range(16)], reads=(xt_res, sl_res), writes=(br,))
                    if pend is not None:
                        pend()
                    pend = evac(c, gi, c0, n, b, br)
            if pend is not None:
                pend()

        u_res = Res("U", seed=snap1)

        def u_views(c, c0, n):
            out = []
            pe_end = min(c0 + n, 1026)
            if c0 < 1026:
                out.append((U[:, c, c0:pe_end], 0, pe_end - c0))
            if c0 + n > 1026:
                s0 = max(c0, 1026)
                ns = c0 + n - s0
                assert (s0 - 1026) % 8 == 0 and ns % 8 == 0
                sq0 = (s0 - 1026) // 8
                out.append((US[:, c, sq0:sq0 + ns // 8, 2:10], s0 - c0, ns))
            return out

        def evac_C(c, gi, c0, n, b, br):
            for dst, p0, pn in u_views(c, c0, n):
                src = bank32(b)[:, p0:p0 + pn]
                if len(dst.shape) == 3:
                    src = src.rearrange("p (s t) -> p s t", t=8)
                P.op("act", ACTF(dst, src, AF.Copy), reads=(br,), writes=(u_res,))

        def evac_H(c, gi, c0, n, b, br):
            for dst, p0, pn in u_views(c, c0, n):
                src = bank32(b)[:, p0:p0 + pn]
                if len(dst.shape) == 3:
                    src = src.rearrange("p (s t) -> p s t", t=8)
                P.op("dve", TT(dst, dst, src, ALU.mult), reads=(br,), writes=(u_res,))

        def post_hook():
            if pending_posts:
                pop_post()
            elif pend_b[0] is not None:
                pend_b[0]()
                pend_b[0] = None

        projB(S_C, TGH, evac_C, hook=post_hook)
        while pending_posts:
            pop_post()
        if pend_b[0] is not None:
            pend_b[0]()
            pend_b[0] = None
        bank_mod[0] = 8
        projB(S_H, TGH, evac_H)
        build_masks2()

        CCV = ar.alloc(F32, [512])
        ccv_res = Res(seed=snap1)
        sp_load(ap_of(CCV, 0, [[1, 512]], 32), cconv, (ccv_res,))
        for c in range(4):
            b, br = nextbank()
            P.group("pe", [TR(bank32(b, 32), ap_of(CCV, c * 128, [[1, 128]], 32), ap_of(IDF, 0, [[1, 32]], 32))],
                    reads=(ccv_res, cst), writes=(br,))
            P.op("act", ACTF(US[:, c, :, 0:2], bank32(b, 32).rearrange("p (s j) -> p s j", j=2), AF.Copy),
                 reads=(br,), writes=(u_res,))

        ACC = ar.view(mix_off + 8 * T * 2, F32, [4, T])
        acc_res = Res("ACC", seed=snap1)
        for c in range(4):
            for (dst, s2, s1, s0) in (
                (ACC[:, c, 0:TP], U[:, c, 2:1026], U[:, c, 1:1025], U[:, c, 0:1024]),
                (ACC[:, c, TP:T].rearrange("p (s t) -> p s t", t=8), US[:, c, :, 2:10], US[:, c, :, 1:9], US[:, c, :, 0:8]),
            ):
                P.op("dve", TSC(dst, s2, CW[:, 2, c:c + 1], ALU.mult), reads=(u_res, cst), writes=(acc_res,))
                P.op("dve", STT(dst, s1, CW[:, 1, c:c + 1], dst, ALU.mult, ALU.add), reads=(u_res, cst), writes=(acc_res,))
                P.op("dve", STT(dst, s0, CW[:, 0, c:c + 1], dst, ALU.mult, ALU.add), reads=(u_res, cst), writes=(acc_res,))

        def evac_B(c, gi, c0, n, b, br):
            t0 = c0 - 2
            P.op("dve", TT(MIX[:, c, t0:t0 + n], bank32(b, n), ACC[:, c, t0:t0 + n], ALU.mult),
                 reads=(br, acc_res), writes=(mix_res,))

        projB(S_B, TGM, evac_B)

        UO = ar.alloc(F32, [512]); UO2 = ar.alloc(F32, [512])
        uo_res = Res(seed=snap1); uo2_res = Res(seed=snap1)
        for c in range(4):
            b, br = nextbank()
            P.group("pe", [TR(bank32(b, 128), U[:, c, 898:1026], IDF)], reads=(u_res, cst), writes=(br,))
            P.op("act", ACTF(UO[:, c * 128:(c + 1) * 128], bank32(b, 128), AF.Copy), reads=(br,), writes=(uo_res,))
            b, br = nextbank()
            P.op("dve", CP(ACC[:, c, 0:32].rearrange("p (s j) -> p s j", j=2), US[:, c, :, 8:10]),
                 reads=(u_res, mix_res), writes=(acc_res,))
            P.group("pe", [TR(bank32(b, 128, 32), ACC[:, c, 0:32], IDF)], reads=(acc_res, cst), writes=(br,))
            P.op("act", ACTF(ap_of(UO2, c * 128, [[1, 128]], 32), bank32(b, 128, 32), AF.Copy), reads=(br,), writes=(uo2_res,))
        sp_store(o_pconv, bass.AP(UO.tensor, UO.offset + 126 * UO.ap[0][0], [[UO.ap[0][0], 2], [1, 512]]), (uo_res,))
        sp_store(o_sconv.rearrange("s j f -> (s j) f"), ap_of(UO2, 0, [[1, 512]], 32), (uo2_res,))

        checkpoint(4)
        snap2 = P.snapshot()
        ar.top = RB
        SQ = [ar.alloc(BF16, [384]) for _ in range(2)]
        RS = [ar.alloc(F32, [384]) for _ in range(2)]
        sq_res = [Res(seed=snap2), Res(seed=snap2)]; rs_res = [Res(seed=snap2), Res(seed=snap2)]
        q_res = Res("QT", seed=snap2); qx_res = Res("QXT", seed=snap2)
        nflip = [0]

        def make_evac_q(dst, dst_res, head0, gcol):
            def evac(c, gi, c0, n, b, br):
                i = nflip[0] % 2
                nflip[0] += 1
                t0 = c0 - 2
                P.op("act", ACTF(SQ[i][:, 0:n], bank32(b, n), AF.Square), reads=(br,), writes=(sq_res[i],))

                def part2():
                    b2, br2 = nextbank()
                    P.group("pe", [MM(bank32(b2, n), ONESB, SQ[i][:, 0:n], True, True)], reads=(sq_res[i], cst), writes=(br2,))
                    P.op("act", ACTF(RS[i][:, 0:n], bank32(b2, n), AF.Ln, scale=1.0 / 128, bias=EPS), reads=(br2,),
                         writes=(rs_res[i],))
                    P.op("act", ACTF(RS[i][:, 0:n], RS[i][:, 0:n], AF.Exp, scale=-0.5), reads=(rs_res[i],), writes=(rs_res[i],))
                    P.op("dve", STT(dst[:, head0 + c, t0:t0 + n], bank32(b, n), gcol, RS[i][:, 0:n], ALU.mult, ALU.mult),
                         reads=(br, rs_res[i], cst), writes=(dst_res,))
                return part2
            return evac

        PT = [ar.alloc(BF16, [512]) for _ in range(4)]
        pt_res = [Res(seed=snap2) for _ in range(4)]
        EX = [ar.alloc(F32, [512]) for _ in range(2)]
        ex_res = [Res(seed=snap2), Res(seed=snap2)]
        RD = [ar.alloc(F32, [512]) for _ in range(2)]
        rd_res = [Res(seed=snap2), Res(seed=snap2)]
        ptc = [0]; exc = [0]; rdc = [0]
        mix_res.r.update(snap2)

        def exp_mask(bank_b, bank_r, n, mask_ap, pt=None):
            if pt is not None:
                P.op("act", ACTF(pt[0][:, 0:n], bank32(bank_b, n), AF.Exp, scale=SCALE), reads=(bank_r,), writes=(pt[1],))
                return pt
            i = ptc[0] % 4; ptc[0] += 1
            if mask_ap is None:
                P.op("act", ACTF(PT[i][:, 0:n], bank32(bank_b, n), AF.Exp, scale=SCALE), reads=(bank_r,), writes=(pt_res[i],))
            else:
                j = exc[0] % 2; exc[0] += 1
                P.op("act", ACTF(EX[j][:, 0:n], bank32(bank_b, n), AF.Exp, scale=SCALE), reads=(bank_r,), writes=(ex_res[j],))
                P.op("dve", TT(PT[i][:, 0:n], EX[j][:, 0:n], mask_ap, ALU.mult), reads=(ex_res[j], cst, msk), writes=(pt_res[i],))
            return PT[i], pt_res[i]

        def finish(bank_o, ro, bank_d, rdn, n, out_ap, in_view=None):
            j = rdc[0] % 2; rdc[0] += 1
            P.op("act", ACTF(RD[j][:, 0:n], bank32(bank_d, n), AF.Ln), reads=(rdn,), writes=(rd_res[j],))
            P.op("act", ACTF(RD[j][:, 0:n], RD[j][:, 0:n], AF.Exp, scale=-1.0), reads=(rd_res[j],), writes=(rd_res[j],))
            o = bank32(bank_o, n)
            r = RD[j][:, 0:n]
            if in_view is not None:
                o = in_view(o); r = in_view(r)
            P.op("dve", TT(out_ap, o, r, ALU.mult), reads=(ro, rd_res[j]), writes=(mix_res,))


        NG = 8
        CMKB = ar.alloc(BF16, [2, 2, 512]); CMVB = ar.alloc(BF16, [2, 2, 512]); CMKT = ar.alloc(BF16, [2, 2, 4, 128])
        cmkb_res = Res(seed=snap2); cmvb_res = Res(seed=snap2); cmkt_res = Res(seed=snap2)
        cmkb_sem = fresh_sem(); cmvb_sem = fresh_sem()
        PX = [ar.alloc(BF16, [128]) for _ in range(2)]
        px_res = [Res(seed=snap2), Res(seed=snap2)]

        def load_cmk(gq):
            P.op("pool", DMA(CMKB, cmk[gq * 2:(gq + 1) * 2].rearrange("s (hf j) f -> j s hf f", j=128)),
                 writes=(cmkb_res,), sem=cmkb_sem)

        def load_cmv(gq):
            P.op("pool", DMA(CMVB, cmv[gq * 2:(gq + 1) * 2].rearrange("s (hf j) f -> j s hf f", j=128)),
                 writes=(cmvb_res,), sem=cmvb_sem)

        def xs_s1(gq):
            for q8 in range(2):
                b, br = nextbank()
                fns = []
                for k in range(8):
                    hf_, h = k // 4, k % 4
                    fns.append(TR(bank16(b)[:, k * 128:(k + 1) * 128], CMKB[:, q8, hf_, h * 128:(h + 1) * 128], IDB))
                P.group("pe", fns, reads=(cmkb_res, cst), writes=(br,))
                eng = evac_engine()
                P.op(eng, copy_op(eng, CMKT[:, q8, :, :, :].rearrange("p a h j -> p (a h) j"),
                                  bank16(b).rearrange("p (k j) -> p k j", k=8)), reads=(br,), writes=(cmkt_res,))
            if gq + 1 < NG:
                load_cmk(gq + 1)
            bs_, rs_ = nextbank()
            fns = []
            for hf_ in range(2):
                for s in range(2):
                    for h in range(4):
                        col = hf_ * 64 + s * 32 + h * 8
                        tok0 = TP + (gq * 2 + s) * 8
                        fns.append(MM(bank32(bs_)[:, col:col + 8], CMKT[:, s, hf_, h, :], QXT[:, h, tok0:tok0 + 8], True, True))
            P.group("pe", fns, reads=(cmkt_res, qx_res), writes=(rs_,))
            return exp_mask(bs_, rs_, 128, None, pt=(PX[gq % 2], px_res[gq % 2]))

        def xs_s2(gq, st):
            px, pxr = st
            bdn, rdn = nextbank(); bo, ro = nextbank()
            P.group("pe", [MM(bank32(bdn, 64), ONESB, px[:, 0:64], True, False),
                           MM(bank32(bdn, 64), ONESB, px[:, 64:128], False, True)], reads=(pxr, cst), writes=(rdn,))
            fns = []
            for s in range(2):
                for h in range(4):
                    col = s * 32 + h * 8
                    fns.append(MM(bank32(bo)[:, col:col + 8], CMVB[:, s, 0, h * 128:(h + 1) * 128], px[:, col:col + 8], True, False))
                    fns.append(MM(bank32(bo)[:, col:col + 8], CMVB[:, s, 1, h * 128:(h + 1) * 128], px[:, 64 + col:64 + col + 8], False, True))
            P.group("pe", fns, reads=(pxr, cmvb_res), writes=(ro,))
            if gq + 1 < NG:
                load_cmv(gq + 1)
            tok0 = TP + gq * 16
            out_ap = ap_of(MIX, 12 * T + tok0, [[8, 2], [T, 4], [1, 8]])
            finish(bo, ro, bdn, rdn, 64, out_ap, in_view=lambda a: a.rearrange("p (s h t) -> p s h t", s=2, h=4))

        xs_state = {}
        xs_calls = []
        for gq in range(NG):
            xs_calls.append(("s1", gq))
            if gq >= 1:
                xs_calls.append(("s2", gq - 1))
        xs_calls.append(("s2", NG - 1))
        xs_pos = [0]

        def xs_hook(k=1):
            for _ in range(k):
                if xs_pos[0] >= len(xs_calls):
                    return
                kind, gq = xs_calls[xs_pos[0]]
                xs_pos[0] += 1
                if kind == "s1":
                    xs_state[gq] = xs_s1(gq)
                else:
                    xs_s2(gq, xs_state.pop(gq))

        hold_after[0] = S_Q1
        projB(S_QX, TGM, make_evac_q(QXT, qx_res, 0, GQX[:, 0:1]))
        load_cmk(0)
        load_cmv(0)
        projB(S_Q0, TGM, make_evac_q(QT, q_res, 0, GQ[:, 0:1]), hook=xs_hook)
        projB(S_Q1, TGM, make_evac_q(QT, q_res, 4, GQ[:, 0:1]), hook=xs_hook)

        checkpoint(5)
        snap3 = P.snapshot()

        CKB = ar.view(xt_off, BF16, [16, 256]); CVB = ar.view(xt_off + 8192, BF16, [16, 256])
        CKT = ar.view(xt_off + 16384, BF16, [16, 2, 128])
        ckb_res = Res(seed=snap3); cvb_res = Res(seed=snap3); ckt_res = Res(seed=snap3)
        late_dma = [
            lambda: P.op("pool", DMA(CKB, ck.rearrange("s j f -> j s f")), writes=(ckb_res,), sem=fresh_sem()),
            lambda: P.op("pool", DMA(CVB, cv.rearrange("s j f -> j s f")), writes=(cvb_res,), sem=fresh_sem()),
            issue_slab_quarter, issue_slab_quarter, issue_slab_quarter, issue_slab_quarter,
        ]
        assert slab_issued[0] == S_OUT[0] and slab_qdone[0] == 0

        def swa_s1(i, h):
            qv = QT[:, 4 * h:4 * h + 4, i * 128:(i + 1) * 128]
            bp, rp = nextbank(); bc, rc = nextbank()
            P.group("pe", [MM(bank32(bp), KT[:, h, i * 128:(i + 1) * 128], qv, True, True)], reads=(kt_res, q_res), writes=(rp,))
            P.group("pe", [MM(bank32(bc), KT[:, h, (i + 1) * 128:(i + 2) * 128], qv, True, True)], reads=(kt_res, q_res), writes=(rc,))
            mprev = (EFIRST if i == 0 else EPREV)[:, h, :, :].rearrange("p g q -> p (g q)")
            mcur = ECUR[:, h, :, :].rearrange("p g q -> p (g q)")
            pp, ppr = exp_mask(bp, rp, 512, mprev)
            pc, pcr = exp_mask(bc, rc, 512, mcur)
            return pp, ppr, pc, pcr

        def swa_s2(i, h, st):
            pp, ppr, pc, pcr = st
            bdn, rdn = nextbank(); bo, ro = nextbank()
            P.group("pe", [MM(bank32(bdn), ONESB, pp, True, False), MM(bank32(bdn), ONESB, pc, False, False),
                           MM(bank32(bdn), ap_of(ONESB, 0, [[1, 128]], 1), ap_of(SROW, h * 512, [[1, 512]], 1), False, True)],
                    reads=(ppr, pcr, cst, msk), writes=(rdn,))
            P.group("pe", [MM(bank32(bo), VT[:, i, h * 128:(h + 1) * 128], pp, True, False),
                           MM(bank32(bo), VT[:, i + 1, h * 128:(h + 1) * 128], pc, False, True)],
                    reads=(ppr, pcr, vt_res), writes=(ro,))
            finish(bo, ro, bdn, rdn, 512, MIX[:, 4 + 4 * h:8 + 4 * h, i * 128:(i + 1) * 128],
                   in_view=lambda a: a.rearrange("p (g q) -> p g q", g=4))

        def xat_s1(h, qc):
            qv = QXT[:, h, qc * 512:(qc + 1) * 512]
            ba, ra = nextbank(); bb, rb = nextbank()
            P.group("pe", [MM(bank32(ba), MKT[:, h, 0:128], qv, True, True)], reads=(mkt_res, qx_res), writes=(ra,))
            P.group("pe", [MM(bank32(bb), MKT[:, h, 128:256], qv, True, True)], reads=(mkt_res, qx_res), writes=(rb,))
            pa, par = exp_mask(ba, ra, 512, None)
            pb, pbr = exp_mask(bb, rb, 512, None)
            return pa, par, pb, pbr

        def xat_s2(h, qc, st):
            pa, par, pb, pbr = st
            bdn, rdn = nextbank(); bo, ro = nextbank()
            P.group("pe", [MM(bank32(bdn), ONESB, pa, True, False), MM(bank32(bdn), ONESB, pb, False, True)],
                    reads=(par, pbr, cst, msk), writes=(rdn,))
            P.group("pe", [MM(bank32(bo), MV[:, 0, h * 128:(h + 1) * 128], pa, True, False),
                           MM(bank32(bo), MV[:, 1, h * 128:(h + 1) * 128], pb, False, True)],
                    reads=(par, pbr, mv_res), writes=(ro,))
            finish(bo, ro, bdn, rdn, 512, MIX[:, 12 + h, qc * 512:(qc + 1) * 512])

        ajobs = [(swa_s1, swa_s2, (i, h)) for i in range(8) for h in range(2)]
        ajobs += [(xat_s1, xat_s2, (h, qc)) for h in range(4) for qc in range(2)]
        prev = None
        for ia, (s1, s2, args) in enumerate(ajobs):
            st = s1(*args)
            if prev is not None:
                prev[0](*prev[1], prev[2])
            prev = (s2, args, st)
            if ia % 3 == 2:
                xs_hook()
                if late_dma:
                    late_dma.pop(0)()
        prev[0](*prev[1], prev[2])
        xs_hook(len(xs_calls))
        while late_dma:
            late_dma.pop(0)()
        hold_after[0] = None
        issue_slabs(S_OUT[1])

        checkpoint(6)
        for s4 in range(4):
            b, br = nextbank()
            fns = []
            for k in range(8):
                s = s4 * 4 + k // 2; h = k % 2
                fns.append(TR(bank16(b)[:, k * 128:(k + 1) * 128], CKB[:, s, h * 128:(h + 1) * 128], IDB))
            P.group("pe", fns, reads=(ckb_res, cst, msk), writes=(br,))
            eng = evac_engine()
            P.op(eng, copy_op(eng, CKT[:, s4 * 4:(s4 + 1) * 4, :, :].rearrange("p s h j -> p (s h) j"),
                              bank16(b).rearrange("p (k j) -> p k j", k=8)), reads=(br,), writes=(ckt_res,))
        for h in range(2):
            bn, rn = nextbank(); bcc, rcc = nextbank()
            q_sgt = ap_of(QT, 4 * h * T + TP, [[8, 16], [T, 4], [1, 8]])
            P.group("pe", [MM(bank32(bn), KT[:, h, 9 * 128:10 * 128], q_sgt, True, True)],
                    reads=(kt_res, q_res), writes=(rn,))
            fns = []
            for s in range(16):
                fns.append(MM(bank32(bcc)[:, s * 32:(s + 1) * 32], CKT[:, s, h, :],
                              QT[:, 4 * h:4 * h + 4, TP + s * 8:TP + s * 8 + 8], True, True))
            P.group("pe", fns, reads=(ckt_res, q_res), writes=(rcc,))
            i = ptc[0] % 4; ptc[0] += 1
            j = exc[0] % 2; exc[0] += 1
            P.op("act", ACTF(EX[j], bank32(bn), AF.Exp, scale=SCALE), reads=(rn,), writes=(ex_res[j],))
            P.op("dve", TT(PT[i].rearrange("p (s g t) -> p s g t", s=16, g=4), EX[j].rearrange("p (s g t) -> p s g t", s=16, g=4),
                           ap_of(ENEW, h * 512, [[8, 16], [128, 4], [1, 8]]), ALU.mult), reads=(ex_res[j], cst, msk), writes=(pt_res[i],))
            pn, pnr = PT[i], pt_res[i]
            i = ptc[0] % 4; ptc[0] += 1
            j = exc[0] % 2; exc[0] += 1
            P.op("act", ACTF(EX[j], bank32(bcc), AF.Exp, scale=SCALE), reads=(rcc,), writes=(ex_res[j],))
            P.op("dve", TT(PT[i].rearrange("p (s g t) -> p s g t", s=16, g=4), EX[j].rearrange("p (s g t) -> p s g t", s=16, g=4),
                           ap_of(EPREV, h * 512, [[0, 16], [128, 4], [1, 8]]), ALU.mult), reads=(ex_res[j], cst, msk), writes=(pt_res[i],))
            pcc, pccr = PT[i], pt_res[i]
            bdn, rdn = nextbank(); bo, ro = nextbank()
            P.group("pe", [MM(bank32(bdn), ONESB, pn, True, False), MM(bank32(bdn), ONESB, pcc, False, False),
                           MM(bank32(bdn), ap_of(ONESB, 0, [[1, 128]], 1), ap_of(SROW, h * 512, [[8, 16], [128, 4], [1, 8]], 1), False, True)],
                    reads=(pnr, pccr, cst, msk), writes=(rdn,))
            fns = [MM(bank32(bo), VT[:, 9, h * 128:(h + 1) * 128], pn, True, False)]
            for s in range(16):
                fns.append(MM(bank32(bo)[:, s * 32:(s + 1) * 32], CVB[:, s, h * 128:(h + 1) * 128], pcc[:, s * 32:(s + 1) * 32],
                              False, s == 15))
            P.group("pe", fns, reads=(pnr, pccr, vt_res, cvb_res), writes=(ro,))
            finish(bo, ro, bdn, rdn, 512, ap_of(MIX, (4 + 4 * h) * T + TP, [[8, 16], [T, 4], [1, 8]]),
                   in_view=lambda a: a.rearrange("p (s g t) -> p s g t", s=16, g=4))

        checkpoint(7)
        snap5 = P.snapshot()
        ar.top = R0

        X1 = ar.alloc(F32, [9, D])
        x1_res = [[Res(f"x1_{i}_{q}", seed=snap5) for q in range(4)] for i in range(9)]
        GLB = ar.alloc(F32, [D])
        XNB2 = [ar.alloc(BF16, [D]) for _ in range(2)]
        JUNK2 = ar.alloc(BF16, [D])
        STAT2 = ar.alloc(F32, [8])
        RL = [ar.alloc(F32, [384]) for _ in range(2)]
        rl_res = [Res(seed=snap5), Res(seed=snap5)]
        c2 = Res("consts2", seed=snap5)
        for og in range(4):
            for i in range(9):
                src = xp[i * 128:(i + 1) * 128, og * 512:(og + 1) * 512] if i < 8 else xs[:, og * 512:(og + 1) * 512]
                sp_load(X1[:, i, og * 512:(og + 1) * 512], src, (x1_res[i][og],))
            if og == 0:
                sp_load(GLB, g_mlp.partition_broadcast(128)[:, 0, :], (c2,))
        xnb2_res = [Res(seed=snap5), Res(seed=snap5)]; junk2_res = Res(seed=snap5)
        stat2_res = [Res(seed=snap5), Res(seed=snap5)]
        xt3_res = [Res(f"xt3_{i}", seed=snap5) for i in range(9)]

        def n3_A(ti):
            sl = ti % 2
            ss = STAT2[:, 2 * sl:2 * sl + 1]; rs = STAT2[:, 2 * sl + 1:2 * sl + 2]
            P.op("act", ACTF(JUNK2, X1[:, ti, :], AF.Square, accum_out=ss), reads=tuple(x1_res[ti]),
                 writes=(junk2_res, stat2_res[sl]))
            P.op("act", ACTF(rs, ss, AF.Ln, scale=1.0 / D, bias=EPS), reads=(stat2_res[sl],), writes=(stat2_res[sl],))
            P.op("act", ACTF(rs, rs, AF.Exp, scale=-0.5), reads=(stat2_res[sl],), writes=(stat2_res[sl],))
            P.op("dve", STT(XNB2[sl], X1[:, ti, :], rs, GLB, ALU.mult, ALU.mult),
                 reads=tuple(x1_res[ti]) + (stat2_res[sl], c2), writes=(xnb2_res[sl],))

        def n3_B(ti):
            sl = ti % 2
            for half in range(2):
                b, br = nextbank()
                fns = [TR(bank16(b)[:, j * 128:(j + 1) * 128], XNB2[sl][:, (half * 8 + j) * 128:(half * 8 + j + 1) * 128], IDB)
                       for j in range(8)]
                P.group("pe", fns, reads=(xnb2_res[sl], cst), writes=(br,))
                eng = evac_engine()
                P.op(eng, copy_op(eng, XT[:, half * 8:half * 8 + 8, 2 + ti * 128:2 + (ti + 1) * 128],
                                  bank16(b).rearrange("p (j t) -> p j t", j=8)), reads=(br,), writes=(xt3_res[ti],))

        for og in range(4):
            sl_ap, sl_res = use_slab(S_OUT[og])
            for ti in range(9):
                b, br = nextbank()
                P.group("pe", [MM(bank32(b), MIX[:, kc, ti * 128:(ti + 1) * 128], sl_ap[:, kc, :], kc == 0, kc == 15)
                               for kc in range(16)], reads=(mix_res, sl_res), writes=(br,))
                xv = X1[:, ti, og * 512:(og + 1) * 512]
                P.op("dve", TT(xv, bank32(b), xv, ALU.add), reads=(br,), writes=(x1_res[ti][og],))
                if og == 3:
                    n3_A(ti)
                    if ti >= 1:
                        n3_B(ti - 1)
        n3_B(8)
        checkpoint(8)

        sp_store(o_sk[:, 0:120, :], ck[:, 8:128, :], ())
        sp_store(o_sv[:, 0:120, :], cv[:, 8:128, :], ())

        A = MIX
        a_res = mix_res
        rlc = [0]
        for j in range(4):
            ups, dns = S_MLP[j]
            for u in range(4):
                sl_ap, sl_res = use_slab(ups[u])
                for c in range(4):
                    m = u * 4 + c
                    for gi in range(3):
                        c0 = 2 + gi * 384
                        b, br = nextbank()
                        P.group("pe", [MM(bank32(b, 384), sl_ap[:, kc, c * 128:(c + 1) * 128], XT[:, kc, c0:c0 + 384], kc == 0, kc == 15)
                                       for kc in range(16)], reads=tuple(xt3_res[3 * gi:3 * gi + 3]) + (sl_res,), writes=(br,))
                        k = rlc[0] % 2; rlc[0] += 1
                        P.op("act", ACTF(RL[k], bank32(b, 384), AF.Relu), reads=(br,), writes=(rl_res[k],))
                        P.op("dve", TT(A[:, m, gi * 384:(gi + 1) * 384], RL[k], RL[k], ALU.mult), reads=(rl_res[k],), writes=(a_res,))
            for og in range(4):
                sl_ap, sl_res = use_slab(dns[og])
                for ti in range(9):
                    b, br = nextbank()
                    P.group("pe", [MM(bank32(b), A[:, kc, ti * 128:(ti + 1) * 128], sl_ap[:, kc, :], kc == 0, kc == 15)
                                   for kc in range(16)], reads=(a_res, sl_res), writes=(br,))
                    xv = X1[:, ti, og * 512:(og + 1) * 512]
                    P.op("dve", TT(xv, bank32(b), xv, ALU.add), reads=(br,), writes=(x1_res[ti][og],))
                    if j == 3:
                        dst = (yp[ti * 128:(ti + 1) * 128, og * 512:(og + 1) * 512] if ti < 8
                               else ys[:, og * 512:(og + 1) * 512])
                        sp_store(dst, xv, (x1_res[ti][og],))

        finalize()
    except _Stop:
        pass
    return nc


_NC_CACHE = {}


def _rel_bucket_np(n):
    n = np.asarray(n)
    max_exact = 16
    nf = np.maximum(n, 1).astype(np.float32)
    large = max_exact + (np.log(nf / np.float32(max_exact)) / np.float32(math.log(128 / max_exact))
                         * np.float32(32 - max_exact)).astype(np.int32)
    large = np.minimum(large, 31)
    return np.where(n < max_exact, n, large)


def kernel(x_prompt, x_sample, mem_prompt, cache_conv, cache_swa_k, cache_swa_v, cache_mem_k, cache_mem_v,
           rel_bias_table, g_mix, w_in, conv_w, g_q_swa, g_k_swa, sinks, g_q_x, g_k_x, g_mem,
           w_mem_k, w_mem_v, w_out, g_mlp, w_up, w_down):
    f = lambda a: np.ascontiguousarray(np.asarray(a, dtype=np.float32))
    if "nc" not in _NC_CACHE:
        _NC_CACHE["nc"] = build()
    nc = _NC_CACHE["nc"]

    x_prompt = f(x_prompt); x_sample = f(x_sample); mem_prompt = f(mem_prompt)
    bk = _rel_bucket_np(np.arange(128))
    oh1 = np.zeros((32, 128), np.float32); oh1[bk, np.arange(128)] = 1.0
    seq = np.arange(128) // 8
    bdm = (seq[:, None] == seq[None, :]).astype(np.float32)
    ident = np.eye(128, dtype=np.float32)

    shared = {
        "w_in": f(w_in[0]), "w_mk": f(w_mem_k[0]), "w_mv": f(w_mem_v[0]), "w_out": f(w_out[0]),
        "w_up": f(w_up[0]), "w_down": f(w_down[0]), "table": f(rel_bias_table), "g_mix": f(g_mix),
        "conv_w": f(conv_w[0]), "g_q": f(g_q_swa), "g_k": f(g_k_swa), "sinks": f(sinks), "g_qx": f(g_q_x),
        "g_kx": f(g_k_x), "g_mem": f(g_mem), "g_mlp": f(g_mlp), "oh1": oh1, "bd": bdm, "ident": ident, "aident": np.ascontiguousarray(ident[::-1]),
    }
    in_maps = []
    for c in range(NCORES):
        b, hf = c // 2, c % 2
        m = dict(shared)
        m["xp"] = f(x_prompt[b, hf * TP:(hf + 1) * TP])
        m["xh"] = f(x_prompt[b, TP - 128:TP]) if hf == 1 else np.zeros((128, D), np.float32)
        m["xs"] = f(x_sample[c * 16:(c + 1) * 16].reshape(TS, D))
        m["xm"] = f(mem_prompt[b])
        m["cconv"] = f(cache_conv[0, c * 16:(c + 1) * 16].reshape(32, 512))
        m["ck"] = f(cache_swa_k[0, c * 16:(c + 1) * 16].reshape(16, 128, 256))
        m["cv"] = f(cache_swa_v[0, c * 16:(c + 1) * 16].reshape(16, 128, 256))
        m["cmk"] = f(cache_mem_k[0, c * 16:(c + 1) * 16].reshape(16, 256, 512))
        m["cmv"] = f(cache_mem_v[0, c * 16:(c + 1) * 16].reshape(16, 256, 512))
        m["hflag"] = np.full((128, 1), float(hf), np.float32)
        in_maps.append(m)

    res = run_bass_kernel_spmd(nc, in_maps, core_ids=list(range(NCORES)))
    R = res.results
    y_prompt = np.stack([np.concatenate([R[2 * b]["yp"], R[2 * b + 1]["yp"]], axis=0) for b in range(4)])
    y_sample = np.concatenate([R[c]["ys"].reshape(16, 8, D) for c in range(NCORES)], axis=0)
    p_conv = np.stack([R[2 * b + 1]["o_pconv"] for b in range(4)])[None]
    p_k = np.stack([R[2 * b + 1]["o_pk"].reshape(128, 2, 128) for b in range(4)])[None]
    p_v = np.stack([R[2 * b + 1]["o_pv"].reshape(128, 2, 128) for b in range(4)])[None]
    p_mk = np.stack([R[2 * b]["o_mk"].reshape(256, 4, 128) for b in range(4)])[None]
    p_mv = np.stack([R[2 * b]["o_mv"].reshape(256, 4, 128) for b in range(4)])[None]
    s_conv = np.concatenate([R[c]["o_sconv"] for c in range(NCORES)], axis=0)[None]
    s_k = np.concatenate([R[c]["o_sk"].reshape(16, 128, 2, 128) for c in range(NCORES)], axis=0)[None]
    s_v = np.concatenate([R[c]["o_sv"].reshape(16, 128, 2, 128) for c in range(NCORES)], axis=0)[None]
    outs = (y_prompt, y_sample, p_conv, p_k, p_v, p_mk, p_mv, s_conv, s_k, s_v)
    return tuple(np.ascontiguousarray(o, dtype=np.float32) for o in outs)
```

```python
import math
from contextlib import ExitStack

import numpy as np
import concourse.bass as bass
import concourse.mybir as mybir
from concourse.bass_utils import run_bass_kernel_spmd

F32 = mybir.dt.float32
BF16 = mybir.dt.bfloat16
AF = mybir.ActivationFunctionType
ALU = mybir.AluOpType

NCORES = 8
D = 2048
TP = 1024
TS = 128
T = TP + TS
XC = T + 2
EPS = 1e-6
SCALE = 128 ** -0.5
ENGS = ("pe", "act", "dve", "pool", "sp")
STOP = 99
SKIP = 0


class _Stop(Exception):
    pass


class Sem:
    def __init__(self, h, step):
        self.h = h
        self.n = 0
        self.step = step


class Res:
    def __init__(self, name="", excl=False, seed=None, multi=False):
        self.name = name
        self.w = None
        self.r = dict(seed) if seed else {}
        self.multi = multi
        self.wset = {}
        self.excl = excl


class Prog:
    def __init__(self, nc, stack):
        self.nc = nc
        self.stack = stack
        self.q = {e: [] for e in ENGS}
        self.waited = {e: {} for e in ENGS}
        self.esem = {e: self.new_sem("prog_" + e, 1) for e in ("pe", "act", "dve", "pool")}
        self.dma_sems = []
        self.pending = {e: [] for e in ENGS}
        self.snap_exclude = set()

    def new_sem(self, name, step):
        h = self.stack.enter_context(self.nc.semaphore(name))
        return Sem(h, step)

    def dma_sem(self, name):
        s = self.new_sem(name, 16)
        self.dma_sems.append(s)
        return s

    def emit(self, eng, fn, deps=(), sem=None, inc=True):
        waits = []
        alld = list(deps) + self.pending[eng]
        self.pending[eng] = []
        for d in alld:
            if d is None:
                continue
            s, v = d
            if self.waited[eng].get(id(s), 0) >= v:
                continue
            self.waited[eng][id(s)] = v
            waits.append((s, v))
        tok = None
        if inc:
            if sem is None:
                sem = self.esem[eng]
            sem.n += sem.step
            tok = (sem, sem.n)
            self.q[eng].append((waits, fn, sem))
        else:
            self.q[eng].append((waits, fn, None))
        return tok

    @staticmethod
    def deps(reads, writes):
        d = []
        for r in reads:
            if r.multi:
                d.extend(r.wset.values())
            else:
                d.append(r.w)
        for r in writes:
            if not r.multi:
                d.append(r.w)
            d.extend(r.r.values())
        return d

    @staticmethod
    def use(tok, reads, writes):
        s, v = tok
        for r in reads:
            o = r.r.get(id(s))
            if o is None or o[1] < v:
                r.r[id(s)] = tok
        for r in writes:
            if r.multi:
                o = r.wset.get(id(s))
                if o is None or o[1] < v:
                    r.wset[id(s)] = tok
            else:
                r.w = tok
                r.r = {}

    @staticmethod
    def _split(reads, writes):
        ex = tuple(r for r in reads if r.excl)
        if ex:
            reads = tuple(r for r in reads if not r.excl)
            writes = tuple(writes) + ex
        return reads, writes

    def op(self, eng, fn, reads=(), writes=(), sem=None):
        reads, writes = self._split(reads, writes)
        tok = self.emit(eng, fn, self.deps(reads, writes), sem)
        self.use(tok, reads, writes)
        return tok

    def group(self, eng, fns, reads=(), writes=()):
        reads, writes = self._split(reads, writes)
        d = self.deps(reads, writes)
        n = len(fns)
        tok = None
        for i, fn in enumerate(fns):
            tok = self.emit(eng, fn, d if i == 0 else (), inc=(i == n - 1))
        self.use(tok, reads, writes)
        return tok

    def snapshot(self):
        snap = {}
        for sm in list(self.esem.values()) + self.dma_sems:
            if sm.n > 0 and id(sm) not in self.snap_exclude:
                snap[id(sm)] = (sm, sm.n)
        return snap

    def barrier(self):
        toks = [(s, s.n) for s in self.esem.values() if s.n > 0]
        toks += [(s, s.n) for s in self.dma_sems if s.n > 0]
        for e in ENGS:
            self.pending[e] = self.pending[e] + toks

    def replay(self, block):
        def run(name, e):
            for waits, fn, sem in self.q[name]:
                for s, v in waits:
                    e.wait_ge(s.h, v)
                ins = fn(e)
                if sem is not None:
                    ins.then_inc(sem.h, sem.step)

        @block.tensor
        def _(e):
            run("pe", e)

        @block.scalar
        def _(e):
            run("act", e)

        @block.vector
        def _(e):
            run("dve", e)

        @block.gpsimd
        def _(e):
            run("pool", e)

        @block.sync
        def _(e):
            run("sp", e)


def MM(out, lhsT, rhs, start, stop):
    return lambda e: e.matmul(out, lhsT=lhsT, rhs=rhs, start=start, stop=stop)


def TR(out, in_, ident):
    return lambda e: e.transpose(out, in_, ident)


def ACTF(out, in_, func, scale=None, bias=None, accum_out=None):
    kw = {}
    if scale is not None:
        kw["scale"] = scale
    if bias is not None:
        kw["bias"] = bias
    if accum_out is not None:
        kw["accum_out"] = accum_out
    return lambda e: e.activation(out=out, in_=in_, func=func, **kw)


def TT(out, in0, in1, op):
    return lambda e: e.tensor_tensor(out=out, in0=in0, in1=in1, op=op)


def TSC(out, in0, s1, op0, s2=None, op1=None):
    if op1 is None:
        return lambda e: e.tensor_scalar(out=out, in0=in0, scalar1=s1, scalar2=None, op0=op0)
    return lambda e: e.tensor_scalar(out=out, in0=in0, scalar1=s1, scalar2=s2, op0=op0, op1=op1)


def STT(out, in0, scalar, in1, op0, op1):
    return lambda e: e.scalar_tensor_tensor(out=out, in0=in0, scalar=scalar, in1=in1, op0=op0, op1=op1)


def CP(out, in_):
    return lambda e: e.tensor_copy(out=out, in_=in_)


def RCP(out, in_):
    return lambda e: e.reciprocal(out=out, in_=in_)


def RCPF(out, in_):
    return lambda e: e.reciprocal_approx_fast(out=out, in_=in_)


def MSET(ap, val):
    return lambda e: e.memset(ap, val)


def DMA(out, in_, **kw):
    kw.setdefault("allow_slow_non_contiguous", True)
    return lambda e: e.dma_start(out=out, in_=in_, **kw)


class Arena:
    def __init__(self, nc, stack, nbytes):
        self.nbytes = nbytes
        self.t32 = stack.enter_context(nc.sbuf_tensor("arena", [128, nbytes // 4], F32))
        self.t16 = self.t32.bitcast(BF16)
        self.top = 0
        self.marks = []

    def alloc(self, dt, shape, np_=128):
        sz = 4 if dt is F32 else 2
        n = 1
        for s in shape:
            n *= s
        nb = (n * sz + 63) // 64 * 64
        off = self.top
        self.top += nb
        assert self.top <= self.nbytes, f"arena overflow {self.top} > {self.nbytes}"
        return self.view(off, dt, shape, np_)

    def view(self, off, dt, shape, np_=128):
        sz = 4 if dt is F32 else 2
        t = self.t32 if dt is F32 else self.t16
        pstep = self.nbytes // sz
        dims = []
        st = 1
        for s in reversed(shape):
            dims.append([st, s])
            st *= s
        dims.reverse()
        return bass.AP(t, off // sz, [[pstep, np_]] + dims)

    def mark(self):
        self.marks.append(self.top)

    def release(self):
        self.top = self.marks.pop()


def ap_of(ap, off_elems, dims, np_=None):
    p = ap.ap[0]
    return bass.AP(ap.tensor, ap.offset + off_elems, [[p[0], p[1] if np_ is None else np_]] + [list(d) for d in dims])


def build():
    nc = bass.Bass("TRN2", target_bir_lowering=False)

    def din(name, shape):
        return nc.dram_tensor(name, list(shape), F32, kind="ExternalInput").ap()

    def dout(name, shape):
        return nc.dram_tensor(name, list(shape), F32, kind="ExternalOutput").ap()

    xp = din("xp", [TP, D]); xh = din("xh", [128, D]); xs = din("xs", [TS, D]); xm = din("xm", [256, D])
    cconv = din("cconv", [32, 512])
    ck = din("ck", [16, 128, 256]); cv = din("cv", [16, 128, 256])
    cmk = din("cmk", [16, 256, 512]); cmv = din("cmv", [16, 256, 512])
    hflag = din("hflag", [128, 1]); oh1 = din("oh1", [32, 128]); bd = din("bd", [128, 128]); identd = din("ident", [128, 128]); aidentd = din("aident", [128, 128])
    w_in = din("w_in", [D, 3584]); w_mk = din("w_mk", [D, 512]); w_mv = din("w_mv", [D, 512])
    w_out = din("w_out", [D, D]); w_up = din("w_up", [D, 8192]); w_down = din("w_down", [8192, D])
    table = din("table", [32, 8]); g_mix = din("g_mix", [1, D]); conv_w = din("conv_w", [3, 512])
    g_q = din("g_q", [1, 128]); g_k = din("g_k", [1, 128]); sinks = din("sinks", [1, 8])
    g_qx = din("g_qx", [1, 128]); g_kx = din("g_kx", [1, 128]); g_mem = din("g_mem", [1, D]); g_mlp = din("g_mlp", [1, D])

    yp = dout("yp", [TP, D]); ys = dout("ys", [TS, D])
    o_pconv = dout("o_pconv", [2, 512]); o_pk = dout("o_pk", [128, 256]); o_pv = dout("o_pv", [128, 256])
    o_mk = dout("o_mk", [256, 512]); o_mv = dout("o_mv", [256, 512])
    o_sconv = dout("o_sconv", [16, 2, 512]); o_sk = dout("o_sk", [16, 128, 256]); o_sv = dout("o_sv", [16, 128, 256])
    fscr = nc.dram_tensor("fscr", [8, 384], F32, kind="Internal").ap()

    stack = ExitStack()
    try:
      with stack:
        P = Prog(nc, stack)

        def finalize():
            P.barrier()
            P.emit("sp", lambda e: e.nop(), inc=False)
            P.emit("act", lambda e: e.nop(), inc=False)
            with nc.Block() as block:
                P.replay(block)

        def checkpoint(k):
            if STOP == k:
                finalize()
                raise _Stop()
        ARENA_BYTES = 207 * 1024
        ar = Arena(nc, stack, ARENA_BYTES)
        ps32_t = stack.enter_context(nc.psum_tensor("ps", [128, 4096], F32))
        ps16_t = ps32_t.bitcast(BF16)

        def bank32(b, n=512, np_=128):
            return bass.AP(ps32_t, b * 512, [[4096, np_], [1, n]])

        def bank16(b, n=1024):
            return bass.AP(ps16_t, b * 1024, [[8192, 128], [1, n]])

        bank_res = [Res(f"bank{i}", excl=True) for i in range(8)]
        bank_ctr = [0]

        bank_mod = [8]

        def nextbank():
            b = bank_ctr[0] % bank_mod[0]
            bank_ctr[0] += 1
            return b, bank_res[b]

        SLAB = [ar.alloc(BF16, [16, 512]) for _ in range(2)]
        slab_res = [Res("slab0"), Res("slab1")]
        slab_sem = [P.dma_sem("slab0"), P.dma_sem("slab1")]
        P.snap_exclude.update(id(x) for x in slab_sem)
        XT = ar.alloc(BF16, [16, XC]); xt_off = ar.top - ((16 * XC * 2 + 63) // 64 * 64)
        MIX = ar.alloc(BF16, [16, T]); mix_off = ar.top - 16 * T * 2
        IDF = ar.alloc(F32, [128]); IDB = ar.alloc(BF16, [128]); ONESB = ar.alloc(BF16, [128])
        R0 = ar.top
        xt_res = Res("XT")
        mix_res = Res("MIX")

        st_sem = P.dma_sem("st")
        semc = [0]

        def fresh_sem():
            semc[0] += 1
            return P.dma_sem(f"d{semc[0]}")

        def sp_load(out, in_, writes, reads=(), sem=None, **kw):
            return P.op("sp", DMA(out, in_, **kw), reads=reads, writes=writes, sem=sem or fresh_sem())

        def sp_store(out, in_, reads, sem=None, **kw):
            return P.op("sp", DMA(out, in_, **kw), reads=reads, writes=(), sem=sem or st_sem)

        def prow(ap, p0, np_, c0, n):
            return bass.AP(ap.tensor, ap.offset + p0 * ap.ap[0][0] + c0, [[ap.ap[0][0], np_], [1, n]])

        slab_list = []

        def add_slab(wap, r0, c0):
            slab_list.append((wap, r0, c0))
            return len(slab_list) - 1

        slab_issued = [0]

        slab_qdone = [0]

        def issue_slab_quarter():
            i = slab_issued[0]
            if i >= len(slab_list):
                return
            wap, r0, c0 = slab_list[i]
            slot = i % 2
            qd = slab_qdone[0]
            src = wap[r0 + qd * 512: r0 + (qd + 1) * 512, c0:c0 + 512].rearrange("(k p) n -> p k n", p=128)
            dst = SLAB[slot][:, qd * 4:(qd + 1) * 4, :]
            P.op("pool", DMA(dst, src), writes=(slab_res[slot],), sem=slab_sem[slot])
            slab_qdone[0] += 1
            if slab_qdone[0] == 4:
                slab_qdone[0] = 0
                slab_issued[0] += 1

        def issue_slabs(upto):
            while slab_issued[0] <= min(upto, len(slab_list) - 1):
                issue_slab_quarter()

        S_MK = add_slab(w_mk, 0, 0)
        S_MV = add_slab(w_mv, 0, 0)
        S_KV = add_slab(w_in, 0, 2560)
        S_C = add_slab(w_in, 0, 512)
        S_H = add_slab(w_in, 0, 1024)
        S_B = add_slab(w_in, 0, 0)
        S_QX = add_slab(w_in, 0, 3072)
        S_Q0 = add_slab(w_in, 0, 1536)
        S_Q1 = add_slab(w_in, 0, 2048)
        S_OUT = [add_slab(w_out, 0, og * 512) for og in range(4)]
        S_MLP = []
        for j in range(4):
            ups = [add_slab(w_up, 0, (j * 4 + u) * 512) for u in range(4)]
            dns = [add_slab(w_down, j * 2048, og * 512) for og in range(4)]
            S_MLP.append((ups, dns))

        hold_after = [None]

        def use_slab(i):
            nxt = i + 1
            if hold_after[0] is not None:
                nxt = min(nxt, hold_after[0])
            issue_slabs(max(nxt, i))
            return SLAB[i % 2], slab_res[i % 2]

        if not (SKIP & 1):
            issue_slabs(1)

        GQ = ar.alloc(F32, [1]); GQX = ar.alloc(F32, [1]); HF = ar.alloc(F32, [1])
        GKB = ar.alloc(F32, [128]); GKXB = ar.alloc(F32, [128])
        CW = ar.alloc(F32, [3, 4])
        BDT = ar.alloc(F32, [128]); AIDF = ar.alloc(F32, [128])
        ECUR = ar.alloc(F32, [2, 4, 128]); EPREV = ar.alloc(F32, [2, 4, 128])
        EFIRST = ar.alloc(F32, [2, 4, 128]); ENEW = ar.alloc(F32, [2, 4, 128])
        SROW = ar.alloc(BF16, [2, 512])
        SMALL = ar.alloc(F32, [64])
        TBL = ar.alloc(F32, [8]); OH1 = ar.alloc(F32, [128]); FPAD = ar.alloc(F32, [384]); SK1 = ar.alloc(F32, [8]); SKE = ar.alloc(F32, [8])
        KT = ar.alloc(BF16, [2, 1280]); VT = ar.alloc(BF16, [10, 256])
        MKT = ar.alloc(BF16, [4, 256]); MV = ar.alloc(BF16, [2, 512])
        STAT = ar.alloc(F32, [32])
        RA = ar.top
        KNS = [ar.alloc(F32, [512]) for _ in range(2)]
        KNB = [ar.alloc(BF16, [512]) for _ in range(2)]
        JUNK128 = ar.alloc(BF16, [128])
        NXS = 4
        XST = [ar.alloc(F32, [D]) for _ in range(NXS)]
        XNB = [ar.alloc(BF16, [D]) for _ in range(2)]
        GB = ar.view(mix_off, F32, [D]); GMB = ar.view(mix_off + 8192, F32, [D])
        XM = ar.view(mix_off + 16384, BF16, [16, 256]); XH = ar.view(mix_off + 24576, BF16, [16, 128])
        JUNK = ar.view(mix_off + 28672, BF16, [D])
        xst_res = [Res() for _ in range(NXS)]; xnb_res = [Res(), Res()]; junk_res = Res(); stat_res = [Res(), Res()]
        xst_sem = [P.dma_sem(f"xst{i}") for i in range(NXS)]
        kns_sem = [P.dma_sem("kns_st0"), P.dma_sem("kns_st1")]
        msk = Res("masks")
        cst = Res("consts", multi=True)

        idr = Res("ident"); gbr = Res("gb"); gmbr = Res("gmb")
        sp_load(XST[0], xm[0:128, :], (xst_res[0],), sem=xst_sem[0])
        sp_load(IDF, identd, (idr, cst))
        sp_load(GMB, g_mem.partition_broadcast(128)[:, 0, :], (gmbr, cst))
        sp_load(XST[1], xm[128:256, :], (xst_res[1],), sem=xst_sem[1])
        sp_load(GB, g_mix.partition_broadcast(128)[:, 0, :], (gbr, cst))
        sp_load(XST[2], xh, (xst_res[2],), sem=xst_sem[2])
        sp_load(XST[3], xp[0:128, :], (xst_res[3],), sem=xst_sem[3])
        P.op("dve", CP(IDB, IDF), reads=(idr,), writes=(idr, cst))
        with nc.allow_non_contiguous_dma(reason="tiny constant loads"):
            sp_load(GQ, g_q.rearrange("o d -> d o"), (cst,))
            sp_load(GQX, g_qx.rearrange("o d -> d o"), (cst,))
            sp_load(HF, hflag, (cst,))
            sp_load(GKB, g_k.partition_broadcast(128)[:, 0, :], (cst,))
            sp_load(GKXB, g_kx.partition_broadcast(128)[:, 0, :], (cst,))
            sp_load(CW, conv_w.rearrange("i (c p) -> p i c", p=128), (cst,))
            sp_load(BDT, bd, (cst,))
            sp_load(AIDF, aidentd, (cst,))
            sp_load(ap_of(TBL, 0, [[1, 8]], 32), table, (cst,))
            sp_load(ap_of(OH1, 0, [[1, 128]], 32), oh1, (cst,))
            sp_load(SK1, sinks.partition_broadcast(128)[:, 0, :], (cst,))
        P.op("dve", MSET(ONESB, 1.0), writes=(cst,))
        def build_masks():
            P.op("dve", MSET(ap_of(FPAD, 0, [[1, 384]], 8), 0.0), writes=(msk,))
            if not (SKIP & 2):
                b, br = nextbank()
                P.group("pe", [MM(bank32(b, 128, 8), ap_of(TBL, 0, [[1, 8]], 32), ap_of(OH1, 0, [[1, 128]], 32), True, True)],
                        reads=(cst, msk), writes=(br,))
                P.op("act", ACTF(ap_of(FPAD, 127, [[1, 128]], 8), bank32(b, 128, 8), AF.Exp), reads=(br,), writes=(msk,))
                fs = Res("fscr")
                P.op("sp", DMA(fscr, ap_of(FPAD, 0, [[1, 384]], 8)), reads=(cst, msk), writes=(fs,), sem=fresh_sem())
                for h in range(2):
                    src_c = bass.AP(fscr.tensor, (4 * h) * 384 + 0, [[1, 128], [384, 4], [1, 128]])
                    src_p = bass.AP(fscr.tensor, (4 * h) * 384 + 128, [[1, 128], [384, 4], [1, 128]])
                    sp_load(ECUR[:, h, :, :], src_c, (msk,), reads=(fs,))
                    sp_load(EPREV[:, h, :, :], src_p, (msk,), reads=(fs,))

        def build_masks2():
            if not (SKIP & 2):
                for E in (ECUR, EPREV):
                    for h in range(2):
                        ev = E[:, h, :, :].rearrange("p g q -> p (g q)")
                        b, br = nextbank()
                        P.group("pe", [MM(bank32(b), AIDF, ev, True, True)], reads=(cst, msk), writes=(br,))
                        P.op("act", ACTF(ev, bank32(b), AF.Copy), reads=(br,), writes=(msk,))
            if not (SKIP & 8):
                P.op("dve", TSC(EFIRST.rearrange("p a g q -> p (a g q)"), EPREV.rearrange("p a g q -> p (a g q)"), HF[:, 0:1], ALU.mult),
                     reads=(cst, msk), writes=(msk,))
            if not (SKIP & 16):
                bd_b = ap_of(BDT, 0, [[0, 8], [1, 128]])
                P.op("dve", TT(ENEW.rearrange("p a g q -> p (a g) q"), ECUR.rearrange("p a g q -> p (a g) q"), bd_b, ALU.mult),
                     reads=(cst, msk), writes=(msk,))
            P.op("act", ACTF(SKE, SK1, AF.Exp), reads=(cst, msk), writes=(msk,))
            for hh in range(8):
                P.op("dve", TSC(SROW[:, hh // 4, (hh % 4) * 128:(hh % 4 + 1) * 128], ONESB, SKE[:, hh:hh + 1], ALU.mult),
                     reads=(cst, msk), writes=(msk,))


        checkpoint(0)

        evac_flip = [0]

        def evac_engine():
            evac_flip[0] ^= 1
            return "act" if evac_flip[0] else "dve"

        def copy_op(eng, out, in_):
            if eng == "act":
                return ACTF(out, in_, AF.Copy)
            return CP(out, in_)

        xm_res = Res("XM"); xh_res = Res("XH")
        tiles0 = [(xm[0:128, :], GMB, XM, xm_res, 0, gmbr), (xm[128:256, :], GMB, XM, xm_res, 128, gmbr),
                  (xh, GB, XH, xh_res, 0, gbr)]
        tiles0 += [(xp[i * 128:(i + 1) * 128, :], GB, XT, xt_res, 2 + i * 128, gbr) for i in range(8)]
        tiles0 += [(xs, GB, XT, xt_res, 2 + TP, gbr)]

        def norm_load(it):
            xs_ = it % NXS
            P.op("act", DMA(XST[xs_], tiles0[it][0]), writes=(xst_res[xs_],), sem=xst_sem[xs_])

        def norm_A(it):
            sl = it % 2
            xs_ = it % NXS
            gb = tiles0[it][1]
            ss = STAT[:, 2 * sl:2 * sl + 1]
            rs = STAT[:, 2 * sl + 1:2 * sl + 2]
            P.op("act", ACTF(JUNK, XST[xs_], AF.Square, accum_out=ss), reads=(xst_res[xs_],), writes=(junk_res, stat_res[sl]))
            P.op("act", ACTF(rs, ss, AF.Ln, scale=1.0 / D, bias=EPS), reads=(stat_res[sl],), writes=(stat_res[sl],))
            P.op("act", ACTF(rs, rs, AF.Exp, scale=-0.5), reads=(stat_res[sl],), writes=(stat_res[sl],))
            P.op("dve", STT(XNB[sl], XST[xs_], rs, gb, ALU.mult, ALU.mult), reads=(xst_res[xs_], stat_res[sl], tiles0[it][5]),
                 writes=(xnb_res[sl],))

        def norm_B(it):
            sl = it % 2
            _, _, dst, dst_res, dst_col0, _ = tiles0[it]
            for half in range(2):
                b, br = nextbank()
                fns = [TR(bank16(b)[:, j * 128:(j + 1) * 128], XNB[sl][:, (half * 8 + j) * 128:(half * 8 + j + 1) * 128], IDB)
                       for j in range(8)]
                P.group("pe", fns, reads=(xnb_res[sl], idr), writes=(br,))
                eng = evac_engine()
                P.op(eng, copy_op(eng, dst[:, half * 8:half * 8 + 8, dst_col0:dst_col0 + 128],
                                  bank16(b).rearrange("p (j t) -> p j t", j=8)),
                     reads=(br,), writes=(dst_res,))
            if it == 2:
                P.op("dve", CP(XT[:, :, 0:2], XH[:, :, 126:128]), reads=(xh_res,), writes=(xt_res,))

        checkpoint(1)
        kns_res = [Res(), Res()]; knb_res = [Res(), Res()]; hst_res = [Res(), Res()]
        mkt_res = Res("MKT"); mv_res = Res("MV")
        kt_res = Res("KT"); vt_res = Res("VT")

        def head_rms_tokmajor(bank_ap, nheads, gbc, out_f32, reads_bank, out_res, k):
            c0 = 8 + 4 * k
            for h in range(nheads):
                P.op("act", ACTF(JUNK128, bank_ap[:, h * 128:(h + 1) * 128], AF.Square,
                                 accum_out=STAT[:, c0 + h:c0 + h + 1]),
                     reads=(reads_bank,), writes=(junk_res, hst_res[k]))
            P.op("act", ACTF(STAT[:, c0:c0 + nheads], STAT[:, c0:c0 + nheads], AF.Ln, scale=1.0 / 128, bias=EPS),
                 reads=(hst_res[k],), writes=(hst_res[k],))
            P.op("act", ACTF(STAT[:, c0:c0 + nheads], STAT[:, c0:c0 + nheads], AF.Exp, scale=-0.5),
                 reads=(hst_res[k],), writes=(hst_res[k],))
            for h in range(nheads):
                P.op("dve", STT(out_f32[:, h * 128:(h + 1) * 128], bank_ap[:, h * 128:(h + 1) * 128],
                                STAT[:, c0 + h:c0 + h + 1], gbc, ALU.mult, ALU.mult),
                     reads=(reads_bank, hst_res[k], cst), writes=(out_res,))

        jobs = []
        jobs += [("mk", mt, S_MK) for mt in range(2)]
        jobs += [("mv", mt, S_MV) for mt in range(2)]
        jobs += [("kv", ti, S_KV) for ti in range(10)]

        def tk_mm(n):
            kind, idx, sidx = jobs[n]
            sl_ap, sl_res = use_slab(sidx)
            if kind in ("mk", "mv"):
                lsrc, lres, c0 = XM, xm_res, idx * 128
            elif idx == 0:
                lsrc, lres, c0 = XH, xh_res, 0
            else:
                lsrc, lres, c0 = XT, xt_res, 2 + (idx - 1) * 128
            b = 5 + n % 3
            br = bank_res[b]
            P.group("pe", [MM(bank32(b), lsrc[:, kc, c0:c0 + 128], sl_ap[:, kc, :], kc == 0, kc == 15)
                           for kc in range(16)], reads=(lres, sl_res), writes=(br,))
            return b, br

        def tk_post(n, st):
            kind, idx, _ = jobs[n]
            b, br = st
            k = n % 2
            KS, KB = KNS[k], KNB[k]
            if kind == "mk":
                head_rms_tokmajor(bank32(b), 4, GKXB, KS, br, kns_res[k], k)
                sp_store(o_mk[idx * 128:(idx + 1) * 128, :], KS, (kns_res[k],), sem=kns_sem[k])
                P.op("act", ACTF(KB, KS, AF.Copy), reads=(kns_res[k],), writes=(knb_res[k],))

                def part_b():
                    b2, br2 = nextbank()
                    P.group("pe", [TR(bank16(b2)[:, h * 128:(h + 1) * 128], KB[:, h * 128:(h + 1) * 128], IDB) for h in range(4)],
                            reads=(knb_res[k], cst), writes=(br2,))
                    P.op("dve", CP(MKT[:, :, idx * 128:(idx + 1) * 128], bank16(b2, 512).rearrange("p (h t) -> p h t", h=4)),
                         reads=(br2,), writes=(mkt_res,))
                return part_b
            elif kind == "mv":
                P.op("act", ACTF(KS, bank32(b), AF.Copy), reads=(br,), writes=(kns_res[k],))
                P.op("dve", CP(MV[:, idx, :], bank32(b)), reads=(br,), writes=(mv_res,))
                sp_store(o_mv[idx * 128:(idx + 1) * 128, :], KS, (kns_res[k],), sem=kns_sem[k])
                return None
            else:
                ti = idx
                head_rms_tokmajor(bank32(b), 2, GKB, KS, br, kns_res[k], k)
                P.op("act", ACTF(VT[:, ti, :], bank32(b)[:, 256:512], AF.Copy), reads=(br,), writes=(vt_res,))
                if ti >= 8:
                    P.op("act", ACTF(KS[:, 256:512], bank32(b)[:, 256:512], AF.Copy), reads=(br,), writes=(kns_res[k],))
                P.op("act", ACTF(KB[:, 0:256], KS[:, 0:256], AF.Copy), reads=(kns_res[k],), writes=(knb_res[k],))
                if ti == 8:
                    sp_store(o_pk, KS[:, 0:256], (kns_res[k],), sem=kns_sem[k])
                    sp_store(o_pv, KS[:, 256:512], (kns_res[k],), sem=kns_sem[k])
                if ti == 9:
                    for sq in range(16):
                        sp_store(o_sk[sq, 120:128, :], prow(KS, sq * 8, 8, 0, 256), (kns_res[k],), sem=kns_sem[k])
                        sp_store(o_sv[sq, 120:128, :], prow(KS, sq * 8, 8, 256, 256), (kns_res[k],), sem=kns_sem[k])
                def part_b():
                    b2, br2 = nextbank()
                    P.group("pe", [TR(bank16(b2)[:, h * 128:(h + 1) * 128], KB[:, h * 128:(h + 1) * 128], IDB) for h in range(2)],
                            reads=(knb_res[k], cst), writes=(br2,))
                    P.op("dve", CP(KT[:, :, ti * 128:(ti + 1) * 128], bank16(b2, 256).rearrange("p (h t) -> p h t", h=2)),
                         reads=(br2,), writes=(kt_res,))
                return part_b

        job_state = {"n": 0}
        pending_posts = []
        bank_mod[0] = 5

        def push_mm():
            n = job_state["n"]
            if n >= len(jobs):
                return
            pending_posts.append((n, tk_mm(n)))
            job_state["n"] = n + 1

        pend_b = [None]

        def pop_post():
            n, st = pending_posts.pop(0)
            bnew = tk_post(n, st)
            if pend_b[0] is not None:
                pend_b[0]()
            pend_b[0] = bnew

        NT0 = len(tiles0)
        norm_A(0)
        for it in range(NT0):
            if it + 1 < NT0:
                norm_A(it + 1)
            norm_B(it)
            if it + NXS < NT0:
                norm_load(it + NXS)
            if it >= 3:
                push_mm()
                if len(pending_posts) > 2:
                    pop_post()
        while job_state["n"] < len(jobs):
            push_mm()
            if len(pending_posts) > 2:
                pop_post()
        checkpoint(2)

        checkpoint(3)
        snap1 = P.snapshot()
        build_masks()
        mix_res.r.update(snap1)
        ar.top = RA
        QT = ar.alloc(BF16, [8, T]); QXT = ar.alloc(BF16, [4, T])
        RB = ar.top
        U = ar.alloc(F32, [4, 1026]); US = ar.alloc(F32, [4, 16, 10])

        TGH = [(0, 386), (386, 384), (770, 384)]
        TGM = [(2, 384), (386, 384), (770, 384)]

        def projB(slab_idx, groups, evac, hook=None):
            sl_ap, sl_res = use_slab(slab_idx)
            pend = None
            for c in range(4):
                if hook is not None and c > 0:
                    hook()
                for gi, (c0, n) in enumerate(groups):
                    b, br = nextbank()
                    P.group("pe", [MM(bank32(b, n), sl_ap[:, kc, c * 128:(c + 1) * 128], XT[:, kc, c0:c0 + n], kc == 0, kc == 15)
                                   for kc in range(16)], reads=(xt_res, sl_res), writes=(br,))
                    if pend is not None:
                        pend()
                    pend = evac(c, gi, c0, n, b, br)
            if pend is not None:
                pend()

        u_res = Res("U", seed=snap1)

        def u_views(c, c0, n):
            out = []
            pe_end = min(c0 + n, 1026)
            if c0 < 1026:
                out.append((U[:, c, c0:pe_end], 0, pe_end - c0))
            if c0 + n > 1026:
                s0 = max(c0, 1026)
                ns = c0 + n - s0
                assert (s0 - 1026) % 8 == 0 and ns % 8 == 0
                sq0 = (s0 - 1026) // 8
                out.append((US[:, c, sq0:sq0 + ns // 8, 2:10], s0 - c0, ns))
            return out

        def evac_C(c, gi, c0, n, b, br):
            for dst, p0, pn in u_views(c, c0, n):
                src = bank32(b)[:, p0:p0 + pn]
                if len(dst.shape) == 3:
                    src = src.rearrange("p (s t) -> p s t", t=8)
                P.op("act", ACTF(dst, src, AF.Copy), reads=(br,), writes=(u_res,))

        def evac_H(c, gi, c0, n, b, br):
            for dst, p0, pn in u_views(c, c0, n):
                src = bank32(b)[:, p0:p0 + pn]
                if len(dst.shape) == 3:
                    src = src.rearrange("p (s t) -> p s t", t=8)
                P.op("dve", TT(dst, dst, src, ALU.mult), reads=(br,), writes=(u_res,))

        def post_hook():
            if pending_posts:
                pop_post()
            elif pend_b[0] is not None:
                pend_b[0]()
                pend_b[0] = None

        projB(S_C, TGH, evac_C, hook=post_hook)
        while pending_posts:
            pop_post()
        if pend_b[0] is not None:
            pend_b[0]()
            pend_b[0] = None
        bank_mod[0] = 8
        projB(S_H, TGH, evac_H)
        build_masks2()

        CCV = ar.alloc(F32, [512])
        ccv_res = Res(seed=snap1)
        sp_load(ap_of(CCV, 0, [[1, 512]], 32), cconv, (ccv_res,))
        for c in range(4):
            b, br = nextbank()
            P.group("pe", [TR(bank32(b, 32), ap_of(CCV, c * 128, [[1, 128]], 32), ap_of(IDF, 0, [[1, 32]], 32))],
                    reads=(ccv_res, cst), writes=(br,))
            P.op("act", ACTF(US[:, c, :, 0:2], bank32(b, 32).rearrange("p (s j) -> p s j", j=2), AF.Copy),
                 reads=(br,), writes=(u_res,))

        ACC = ar.view(mix_off + 8 * T * 2, F32, [4, T])
        acc_res = Res("ACC", seed=snap1)
        for c in range(4):
            for (dst, s2, s1, s0) in (
                (ACC[:, c, 0:TP], U[:, c, 2:1026], U[:, c, 1:1025], U[:, c, 0:1024]),
                (ACC[:, c, TP:T].rearrange("p (s t) -> p s t", t=8), US[:, c, :, 2:10], US[:, c, :, 1:9], US[:, c, :, 0:8]),
            ):
                P.op("dve", TSC(dst, s2, CW[:, 2, c:c + 1], ALU.mult), reads=(u_res, cst), writes=(acc_res,))
                P.op("dve", STT(dst, s1, CW[:, 1, c:c + 1], dst, ALU.mult, ALU.add), reads=(u_res, cst), writes=(acc_res,))
                P.op("dve", STT(dst, s0, CW[:, 0, c:c + 1], dst, ALU.mult, ALU.add), reads=(u_res, cst), writes=(acc_res,))

        def evac_B(c, gi, c0, n, b, br):
            t0 = c0 - 2
            P.op("dve", TT(MIX[:, c, t0:t0 + n], bank32(b, n), ACC[:, c, t0:t0 + n], ALU.mult),
                 reads=(br, acc_res), writes=(mix_res,))

        projB(S_B, TGM, evac_B)

        UO = ar.alloc(F32, [512]); UO2 = ar.alloc(F32, [512])
        uo_res = Res(seed=snap1); uo2_res = Res(seed=snap1)
        for c in range(4):
            b, br = nextbank()
            P.group("pe", [TR(bank32(b, 128), U[:, c, 898:1026], IDF)], reads=(u_res, cst), writes=(br,))
            P.op("act", ACTF(UO[:, c * 128:(c + 1) * 128], bank32(b, 128), AF.Copy), reads=(br,), writes=(uo_res,))
            b, br = nextbank()
            P.op("dve", CP(ACC[:, c, 0:32].rearrange("p (s j) -> p s j", j=2), US[:, c, :, 8:10]),
                 reads=(u_res, mix_res), writes=(acc_res,))
            P.group("pe", [TR(bank32(b, 128, 32), ACC[:, c, 0:32], IDF)], reads=(acc_res, cst), writes=(br,))
            P.op("act", ACTF(ap_of(UO2, c * 128, [[1, 128]], 32), bank32(b, 128, 32), AF.Copy), reads=(br,), writes=(uo2_res,))
        sp_store(o_pconv, bass.AP(UO.tensor, UO.offset + 126 * UO.ap[0][0], [[UO.ap[0][0], 2], [1, 512]]), (uo_res,))
        sp_store(o_sconv.rearrange("s j f -> (s j) f"), ap_of(UO2, 0, [[1, 512]], 32), (uo2_res,))

        checkpoint(4)
        snap2 = P.snapshot()
        ar.top = RB
        SQ = [ar.alloc(BF16, [384]) for _ in range(2)]
        RS = [ar.alloc(F32, [384]) for _ in range(2)]
        sq_res = [Res(seed=snap2), Res(seed=snap2)]; rs_res = [Res(seed=snap2), Res(seed=snap2)]
        q_res = Res("QT", seed=snap2); qx_res = Res("QXT", seed=snap2)
        nflip = [0]

        def make_evac_q(dst, dst_res, head0, gcol):
            def evac(c, gi, c0, n, b, br):
                i = nflip[0] % 2
                nflip[0] += 1
                t0 = c0 - 2
                P.op("act", ACTF(SQ[i][:, 0:n], bank32(b, n), AF.Square), reads=(br,), writes=(sq_res[i],))

                def part2():
                    b2, br2 = nextbank()
                    P.group("pe", [MM(bank32(b2, n), ONESB, SQ[i][:, 0:n], True, True)], reads=(sq_res[i], cst), writes=(br2,))
                    P.op("act", ACTF(RS[i][:, 0:n], bank32(b2, n), AF.Ln, scale=1.0 / 128, bias=EPS), reads=(br2,),
                         writes=(rs_res[i],))
                    P.op("act", ACTF(RS[i][:, 0:n], RS[i][:, 0:n], AF.Exp, scale=-0.5), reads=(rs_res[i],), writes=(rs_res[i],))
                    P.op("dve", STT(dst[:, head0 + c, t0:t0 + n], bank32(b, n), gcol, RS[i][:, 0:n], ALU.mult, ALU.mult),
                         reads=(br, rs_res[i], cst), writes=(dst_res,))
                return part2
            return evac

        PT = [ar.alloc(BF16, [512]) for _ in range(4)]
        pt_res = [Res(seed=snap2) for _ in range(4)]
        EX = [ar.alloc(F32, [512]) for _ in range(2)]
        ex_res = [Res(seed=snap2), Res(seed=snap2)]
        RD = [ar.alloc(F32, [512]) for _ in range(2)]
        rd_res = [Res(seed=snap2), Res(seed=snap2)]
        ptc = [0]; exc = [0]; rdc = [0]
        mix_res.r.update(snap2)

        def exp_mask(bank_b, bank_r, n, mask_ap, pt=None):
            if pt is not None:
                P.op("act", ACTF(pt[0][:, 0:n], bank32(bank_b, n), AF.Exp, scale=SCALE), reads=(bank_r,), writes=(pt[1],))
                return pt
            i = ptc[0] % 4; ptc[0] += 1
            if mask_ap is None:
                P.op("act", ACTF(PT[i][:, 0:n], bank32(bank_b, n), AF.Exp, scale=SCALE), reads=(bank_r,), writes=(pt_res[i],))
            else:
                j = exc[0] % 2; exc[0] += 1
                P.op("act", ACTF(EX[j][:, 0:n], bank32(bank_b, n), AF.Exp, scale=SCALE), reads=(bank_r,), writes=(ex_res[j],))
                P.op("dve", TT(PT[i][:, 0:n], EX[j][:, 0:n], mask_ap, ALU.mult), reads=(ex_res[j], cst, msk), writes=(pt_res[i],))
            return PT[i], pt_res[i]

        def finish(bank_o, ro, bank_d, rdn, n, out_ap, in_view=None):
            j = rdc[0] % 2; rdc[0] += 1
            P.op("act", ACTF(RD[j][:, 0:n], bank32(bank_d, n), AF.Ln), reads=(rdn,), writes=(rd_res[j],))
            P.op("act", ACTF(RD[j][:, 0:n], RD[j][:, 0:n], AF.Exp, scale=-1.0), reads=(rd_res[j],), writes=(rd_res[j],))
            o = bank32(bank_o, n)
            r = RD[j][:, 0:n]
            if in_view is not None:
                o = in_view(o); r = in_view(r)
            P.op("dve", TT(out_ap, o, r, ALU.mult), reads=(ro, rd_res[j]), writes=(mix_res,))


        NG = 8
        CMKB = ar.alloc(BF16, [2, 2, 512]); CMVB = ar.alloc(BF16, [2, 2, 512]); CMKT = ar.alloc(BF16, [2, 2, 4, 128])
        cmkb_res = Res(seed=snap2); cmvb_res = Res(seed=snap2); cmkt_res = Res(seed=snap2)
        cmkb_sem = fresh_sem(); cmvb_sem = fresh_sem()
        PX = [ar.alloc(BF16, [128]) for _ in range(2)]
        px_res = [Res(seed=snap2), Res(seed=snap2)]

        def load_cmk(gq):
            P.op("pool", DMA(CMKB, cmk[gq * 2:(gq + 1) * 2].rearrange("s (hf j) f -> j s hf f", j=128)),
                 writes=(cmkb_res,), sem=cmkb_sem)

        def load_cmv(gq):
            P.op("pool", DMA(CMVB, cmv[gq * 2:(gq + 1) * 2].rearrange("s (hf j) f -> j s hf f", j=128)),
                 writes=(cmvb_res,), sem=cmvb_sem)

        def xs_a(gq):
            for q8 in range(2):
                b, br = nextbank()
                fns = []
                for k in range(8):
                    hf_, h = k // 4, k % 4
                    fns.append(TR(bank16(b)[:, k * 128:(k + 1) * 128], CMKB[:, q8, hf_, h * 128:(h + 1) * 128], IDB))
                P.group("pe", fns, reads=(cmkb_res, cst), writes=(br,))
                eng = evac_engine()
                P.op(eng, copy_op(eng, CMKT[:, q8, :, :, :].rearrange("p a h j -> p (a h) j"),
                                  bank16(b).rearrange("p (k j) -> p k j", k=8)), reads=(br,), writes=(cmkt_res,))
            if gq + 1 < NG:
                load_cmk(gq + 1)

        def xs_b(gq):
            bs_, rs_ = nextbank()
            fns = []
            for hf_ in range(2):
                for s in range(2):
                    for h in range(4):
                        col = hf_ * 64 + s * 32 + h * 8
                        tok0 = TP + (gq * 2 + s) * 8
                        fns.append(MM(bank32(bs_)[:, col:col + 8], CMKT[:, s, hf_, h, :], QXT[:, h, tok0:tok0 + 8], True, True))
            P.group("pe", fns, reads=(cmkt_res, qx_res), writes=(rs_,))
            return exp_mask(bs_, rs_, 128, None, pt=(PX[gq % 2], px_res[gq % 2]))

        def xs_s2(gq, st):
            px, pxr = st
            bdn, rdn = nextbank(); bo, ro = nextbank()
            P.group("pe", [MM(bank32(bdn, 64), ONESB, px[:, 0:64], True, False),
                           MM(bank32(bdn, 64), ONESB, px[:, 64:128], False, True)], reads=(pxr, cst), writes=(rdn,))
            fns = []
            for s in range(2):
                for h in range(4):
                    col = s * 32 + h * 8
                    fns.append(MM(bank32(bo)[:, col:col + 8], CMVB[:, s, 0, h * 128:(h + 1) * 128], px[:, col:col + 8], True, False))
                    fns.append(MM(bank32(bo)[:, col:col + 8], CMVB[:, s, 1, h * 128:(h + 1) * 128], px[:, 64 + col:64 + col + 8], False, True))
            P.group("pe", fns, reads=(pxr, cmvb_res), writes=(ro,))
            if gq + 1 < NG:
                load_cmv(gq + 1)
            tok0 = TP + gq * 16
            out_ap = ap_of(MIX, 12 * T + tok0, [[8, 2], [T, 4], [1, 8]])
            finish(bo, ro, bdn, rdn, 64, out_ap, in_view=lambda a: a.rearrange("p (s h t) -> p s h t", s=2, h=4))

        xs_state = {}
        xs_calls = []
        xs_calls.append(("a", 0))
        for gq in range(NG):
            xs_calls.append(("b", gq))
            if gq + 1 < NG:
                xs_calls.append(("a", gq + 1))
            xs_calls.append(("s2", gq))
        xs_pos = [0]

        def xs_hook(k=1):
            for _ in range(k):
                if xs_pos[0] >= len(xs_calls):
                    return
                kind, gq = xs_calls[xs_pos[0]]
                xs_pos[0] += 1
                if kind == "a":
                    xs_a(gq)
                elif kind == "b":
                    xs_state[gq] = xs_b(gq)
                else:
                    xs_s2(gq, xs_state.pop(gq))

        hold_after[0] = S_Q1
        projB(S_QX, TGM, make_evac_q(QXT, qx_res, 0, GQX[:, 0:1]))
        load_cmk(0)
        load_cmv(0)
        projB(S_Q0, TGM, make_evac_q(QT, q_res, 0, GQ[:, 0:1]), hook=xs_hook)
        projB(S_Q1, TGM, make_evac_q(QT, q_res, 4, GQ[:, 0:1]), hook=lambda: xs_hook(2))

        checkpoint(5)
        snap3 = P.snapshot()

        CKB = ar.view(xt_off, BF16, [16, 256]); CVB = ar.view(xt_off + 8192, BF16, [16, 256])
        CKT = ar.view(xt_off + 16384, BF16, [16, 2, 128])
        ckb_res = Res(seed=snap3); cvb_res = Res(seed=snap3); ckt_res = Res(seed=snap3)
        late_dma = [
            lambda: P.op("pool", DMA(CKB, ck.rearrange("s j f -> j s f")), writes=(ckb_res,), sem=fresh_sem()),
            lambda: P.op("pool", DMA(CVB, cv.rearrange("s j f -> j s f")), writes=(cvb_res,), sem=fresh_sem()),
            issue_slab_quarter, issue_slab_quarter, issue_slab_quarter, issue_slab_quarter,
        ]
        assert slab_issued[0] == S_OUT[0] and slab_qdone[0] == 0

        def swa_s1(i, h):
            qv = QT[:, 4 * h:4 * h + 4, i * 128:(i + 1) * 128]
            bp, rp = nextbank(); bc, rc = nextbank()
            P.group("pe", [MM(bank32(bp), KT[:, h, i * 128:(i + 1) * 128], qv, True, True)], reads=(kt_res, q_res), writes=(rp,))
            P.group("pe", [MM(bank32(bc), KT[:, h, (i + 1) * 128:(i + 2) * 128], qv, True, True)], reads=(kt_res, q_res), writes=(rc,))
            mprev = (EFIRST if i == 0 else EPREV)[:, h, :, :].rearrange("p g q -> p (g q)")
            mcur = ECUR[:, h, :, :].rearrange("p g q -> p (g q)")
            pp, ppr = exp_mask(bp, rp, 512, mprev)
            pc, pcr = exp_mask(bc, rc, 512, mcur)
            return pp, ppr, pc, pcr

        def swa_s2(i, h, st):
            pp, ppr, pc, pcr = st
            bdn, rdn = nextbank(); bo, ro = nextbank()
            P.group("pe", [MM(bank32(bdn), ONESB, pp, True, False), MM(bank32(bdn), ONESB, pc, False, False),
                           MM(bank32(bdn), ap_of(ONESB, 0, [[1, 128]], 1), ap_of(SROW, h * 512, [[1, 512]], 1), False, True)],
                    reads=(ppr, pcr, cst, msk), writes=(rdn,))
            P.group("pe", [MM(bank32(bo), VT[:, i, h * 128:(h + 1) * 128], pp, True, False),
                           MM(bank32(bo), VT[:, i + 1, h * 128:(h + 1) * 128], pc, False, True)],
                    reads=(ppr, pcr, vt_res), writes=(ro,))
            finish(bo, ro, bdn, rdn, 512, MIX[:, 4 + 4 * h:8 + 4 * h, i * 128:(i + 1) * 128],
                   in_view=lambda a: a.rearrange("p (g q) -> p g q", g=4))

        def xat_s1(h, qc):
            qv = QXT[:, h, qc * 512:(qc + 1) * 512]
            ba, ra = nextbank(); bb, rb = nextbank()
            P.group("pe", [MM(bank32(ba), MKT[:, h, 0:128], qv, True, True)], reads=(mkt_res, qx_res), writes=(ra,))
            P.group("pe", [MM(bank32(bb), MKT[:, h, 128:256], qv, True, True)], reads=(mkt_res, qx_res), writes=(rb,))
            pa, par = exp_mask(ba, ra, 512, None)
            pb, pbr = exp_mask(bb, rb, 512, None)
            return pa, par, pb, pbr

        def xat_s2(h, qc, st):
            pa, par, pb, pbr = st
            bdn, rdn = nextbank(); bo, ro = nextbank()
            P.group("pe", [MM(bank32(bdn), ONESB, pa, True, False), MM(bank32(bdn), ONESB, pb, False, True)],
                    reads=(par, pbr, cst, msk), writes=(rdn,))
            P.group("pe", [MM(bank32(bo), MV[:, 0, h * 128:(h + 1) * 128], pa, True, False),
                           MM(bank32(bo), MV[:, 1, h * 128:(h + 1) * 128], pb, False, True)],
                    reads=(par, pbr, mv_res), writes=(ro,))
            finish(bo, ro, bdn, rdn, 512, MIX[:, 12 + h, qc * 512:(qc + 1) * 512])

        ajobs = [(swa_s1, swa_s2, (i, h)) for i in range(8) for h in range(2)]
        ajobs += [(xat_s1, xat_s2, (h, qc)) for h in range(4) for qc in range(2)]
        prev = None
        for ia, (s1, s2, args) in enumerate(ajobs):
            st = s1(*args)
            if prev is not None:
                prev[0](*prev[1], prev[2])
            prev = (s2, args, st)
            if ia % 2 == 1:
                xs_hook()
            if ia % 3 == 2:
                if late_dma:
                    late_dma.pop(0)()
        prev[0](*prev[1], prev[2])
        xs_hook(len(xs_calls))
        while late_dma:
            late_dma.pop(0)()
        hold_after[0] = None
        issue_slabs(S_OUT[1])

        checkpoint(6)
        for s4 in range(4):
            b, br = nextbank()
            fns = []
            for k in range(8):
                s = s4 * 4 + k // 2; h = k % 2
                fns.append(TR(bank16(b)[:, k * 128:(k + 1) * 128], CKB[:, s, h * 128:(h + 1) * 128], IDB))
            P.group("pe", fns, reads=(ckb_res, cst, msk), writes=(br,))
            eng = evac_engine()
            P.op(eng, copy_op(eng, CKT[:, s4 * 4:(s4 + 1) * 4, :, :].rearrange("p s h j -> p (s h) j"),
                              bank16(b).rearrange("p (k j) -> p k j", k=8)), reads=(br,), writes=(ckt_res,))
        for h in range(2):
            bn, rn = nextbank(); bcc, rcc = nextbank()
            q_sgt = ap_of(QT, 4 * h * T + TP, [[8, 16], [T, 4], [1, 8]])
            P.group("pe", [MM(bank32(bn), KT[:, h, 9 * 128:10 * 128], q_sgt, True, True)],
                    reads=(kt_res, q_res), writes=(rn,))
            fns = []
            for s in range(16):
                fns.append(MM(bank32(bcc)[:, s * 32:(s + 1) * 32], CKT[:, s, h, :],
                              QT[:, 4 * h:4 * h + 4, TP + s * 8:TP + s * 8 + 8], True, True))
            P.group("pe", fns, reads=(ckt_res, q_res), writes=(rcc,))
            i = ptc[0] % 4; ptc[0] += 1
            j = exc[0] % 2; exc[0] += 1
            P.op("act", ACTF(EX[j], bank32(bn), AF.Exp, scale=SCALE), reads=(rn,), writes=(ex_res[j],))
            P.op("dve", TT(PT[i].rearrange("p (s g t) -> p s g t", s=16, g=4), EX[j].rearrange("p (s g t) -> p s g t", s=16, g=4),
                           ap_of(ENEW, h * 512, [[8, 16], [128, 4], [1, 8]]), ALU.mult), reads=(ex_res[j], cst, msk), writes=(pt_res[i],))
            pn, pnr = PT[i], pt_res[i]
            i = ptc[0] % 4; ptc[0] += 1
            j = exc[0] % 2; exc[0] += 1
            P.op("act", ACTF(EX[j], bank32(bcc), AF.Exp, scale=SCALE), reads=(rcc,), writes=(ex_res[j],))
            P.op("dve", TT(PT[i].rearrange("p (s g t) -> p s g t", s=16, g=4), EX[j].rearrange("p (s g t) -> p s g t", s=16, g=4),
                           ap_of(EPREV, h * 512, [[0, 16], [128, 4], [1, 8]]), ALU.mult), reads=(ex_res[j], cst, msk), writes=(pt_res[i],))
            pcc, pccr = PT[i], pt_res[i]
            bdn, rdn = nextbank(); bo, ro = nextbank()
            P.group("pe", [MM(bank32(bdn), ONESB, pn, True, False), MM(bank32(bdn), ONESB, pcc, False, False),
                           MM(bank32(bdn), ap_of(ONESB, 0, [[1, 128]], 1), ap_of(SROW, h * 512, [[8, 16], [128, 4], [1, 8]], 1), False, True)],
                    reads=(pnr, pccr, cst, msk), writes=(rdn,))
            fns = [MM(bank32(bo), VT[:, 9, h * 128:(h + 1) * 128], pn, True, False)]
            for s in range(16):
                fns.append(MM(bank32(bo)[:, s * 32:(s + 1) * 32], CVB[:, s, h * 128:(h + 1) * 128], pcc[:, s * 32:(s + 1) * 32],
                              False, s == 15))
            P.group("pe", fns, reads=(pnr, pccr, vt_res, cvb_res), writes=(ro,))
            finish(bo, ro, bdn, rdn, 512, ap_of(MIX, (4 + 4 * h) * T + TP, [[8, 16], [T, 4], [1, 8]]),
                   in_view=lambda a: a.rearrange("p (s g t) -> p s g t", s=16, g=4))

        checkpoint(7)
        snap5 = P.snapshot()
        ar.top = R0

        X1 = ar.alloc(F32, [9, D])
        x1_res = [[Res(f"x1_{i}_{q}", seed=snap5) for q in range(4)] for i in range(9)]
        GLB = ar.alloc(F32, [D])
        XNB2 = [ar.alloc(BF16, [D]) for _ in range(2)]
        JUNK2 = ar.alloc(BF16, [D])
        STAT2 = ar.alloc(F32, [8])
        RL = [ar.alloc(F32, [384]) for _ in range(2)]
        rl_res = [Res(seed=snap5), Res(seed=snap5)]
        c2 = Res("consts2", seed=snap5)
        for og in range(4):
            for i in range(9):
                src = xp[i * 128:(i + 1) * 128, og * 512:(og + 1) * 512] if i < 8 else xs[:, og * 512:(og + 1) * 512]
                sp_load(X1[:, i, og * 512:(og + 1) * 512], src, (x1_res[i][og],))
            if og == 0:
                sp_load(GLB, g_mlp.partition_broadcast(128)[:, 0, :], (c2,))
        xnb2_res = [Res(seed=snap5), Res(seed=snap5)]; junk2_res = Res(seed=snap5)
        stat2_res = [Res(seed=snap5), Res(seed=snap5)]
        xt3_res = [Res(f"xt3_{i}", seed=snap5) for i in range(9)]

        def n3_A(ti):
            sl = ti % 2
            ss = STAT2[:, 2 * sl:2 * sl + 1]; rs = STAT2[:, 2 * sl + 1:2 * sl + 2]
            P.op("act", ACTF(JUNK2, X1[:, ti, :], AF.Square, accum_out=ss), reads=tuple(x1_res[ti]),
                 writes=(junk2_res, stat2_res[sl]))
            P.op("act", ACTF(rs, ss, AF.Ln, scale=1.0 / D, bias=EPS), reads=(stat2_res[sl],), writes=(stat2_res[sl],))
            P.op("act", ACTF(rs, rs, AF.Exp, scale=-0.5), reads=(stat2_res[sl],), writes=(stat2_res[sl],))
            P.op("dve", STT(XNB2[sl], X1[:, ti, :], rs, GLB, ALU.mult, ALU.mult),
                 reads=tuple(x1_res[ti]) + (stat2_res[sl], c2), writes=(xnb2_res[sl],))

        def n3_B(ti):
            sl = ti % 2
            for half in range(2):
                b, br = nextbank()
                fns = [TR(bank16(b)[:, j * 128:(j + 1) * 128], XNB2[sl][:, (half * 8 + j) * 128:(half * 8 + j + 1) * 128], IDB)
                       for j in range(8)]
                P.group("pe", fns, reads=(xnb2_res[sl], cst), writes=(br,))
                eng = evac_engine()
                P.op(eng, copy_op(eng, XT[:, half * 8:half * 8 + 8, 2 + ti * 128:2 + (ti + 1) * 128],
                                  bank16(b).rearrange("p (j t) -> p j t", j=8)), reads=(br,), writes=(xt3_res[ti],))

        for og in range(4):
            sl_ap, sl_res = use_slab(S_OUT[og])
            for ti in range(9):
                b, br = nextbank()
                P.group("pe", [MM(bank32(b), MIX[:, kc, ti * 128:(ti + 1) * 128], sl_ap[:, kc, :], kc == 0, kc == 15)
                               for kc in range(16)], reads=(mix_res, sl_res), writes=(br,))
                xv = X1[:, ti, og * 512:(og + 1) * 512]
                P.op("dve", TT(xv, bank32(b), xv, ALU.add), reads=(br,), writes=(x1_res[ti][og],))
                if og == 3:
                    n3_A(ti)
                    if ti >= 1:
                        n3_B(ti - 1)
        n3_B(8)
        checkpoint(8)

        sp_store(o_sk[:, 0:120, :], ck[:, 8:128, :], ())
        sp_store(o_sv[:, 0:120, :], cv[:, 8:128, :], ())

        A = MIX
        a_res = mix_res
        rlc = [0]
        for j in range(4):
            ups, dns = S_MLP[j]
            for u in range(4):
                sl_ap, sl_res = use_slab(ups[u])
                for c in range(4):
                    m = u * 4 + c
                    for gi in range(3):
                        c0 = 2 + gi * 384
                        b, br = nextbank()
                        P.group("pe", [MM(bank32(b, 384), sl_ap[:, kc, c * 128:(c + 1) * 128], XT[:, kc, c0:c0 + 384], kc == 0, kc == 15)
                                       for kc in range(16)], reads=tuple(xt3_res[3 * gi:3 * gi + 3]) + (sl_res,), writes=(br,))
                        k = rlc[0] % 2; rlc[0] += 1
                        P.op("act", ACTF(RL[k], bank32(b, 384), AF.Relu), reads=(br,), writes=(rl_res[k],))
                        P.op("dve", TT(A[:, m, gi * 384:(gi + 1) * 384], RL[k], RL[k], ALU.mult), reads=(rl_res[k],), writes=(a_res,))
            for og in range(4):
                sl_ap, sl_res = use_slab(dns[og])
                for ti in range(9):
                    b, br = nextbank()
                    P.group("pe", [MM(bank32(b), A[:, kc, ti * 128:(ti + 1) * 128], sl_ap[:, kc, :], kc == 0, kc == 15)
                                   for kc in range(16)], reads=(a_res, sl_res), writes=(br,))
                    xv = X1[:, ti, og * 512:(og + 1) * 512]
                    P.op("dve", TT(xv, bank32(b), xv, ALU.add), reads=(br,), writes=(x1_res[ti][og],))
                    if j == 3:
                        dst = (yp[ti * 128:(ti + 1) * 128, og * 512:(og + 1) * 512] if ti < 8
                               else ys[:, og * 512:(og + 1) * 512])
                        sp_store(dst, xv, (x1_res[ti][og],))

        finalize()
    except _Stop:
        pass
    return nc


_NC_CACHE = {}


def _rel_bucket_np(n):
    n = np.asarray(n)
    max_exact = 16
    nf = np.maximum(n, 1).astype(np.float32)
    large = max_exact + (np.log(nf / np.float32(max_exact)) / np.float32(math.log(128 / max_exact))
                         * np.float32(32 - max_exact)).astype(np.int32)
    large = np.minimum(large, 31)
    return np.where(n < max_exact, n, large)


def kernel(x_prompt, x_sample, mem_prompt, cache_conv, cache_swa_k, cache_swa_v, cache_mem_k, cache_mem_v,
           rel_bias_table, g_mix, w_in, conv_w, g_q_swa, g_k_swa, sinks, g_q_x, g_k_x, g_mem,
           w_mem_k, w_mem_v, w_out, g_mlp, w_up, w_down):
    f = lambda a: np.ascontiguousarray(np.asarray(a, dtype=np.float32))
    if "nc" not in _NC_CACHE:
        _NC_CACHE["nc"] = build()
    nc = _NC_CACHE["nc"]

    x_prompt = f(x_prompt); x_sample = f(x_sample); mem_prompt = f(mem_prompt)
    bk = _rel_bucket_np(np.arange(128))
    oh1 = np.zeros((32, 128), np.float32); oh1[bk, np.arange(128)] = 1.0
    seq = np.arange(128) // 8
    bdm = (seq[:, None] == seq[None, :]).astype(np.float32)
    ident = np.eye(128, dtype=np.float32)

    shared = {
        "w_in": f(w_in[0]), "w_mk": f(w_mem_k[0]), "w_mv": f(w_mem_v[0]), "w_out": f(w_out[0]),
        "w_up": f(w_up[0]), "w_down": f(w_down[0]), "table": f(rel_bias_table), "g_mix": f(g_mix),
        "conv_w": f(conv_w[0]), "g_q": f(g_q_swa), "g_k": f(g_k_swa), "sinks": f(sinks), "g_qx": f(g_q_x),
        "g_kx": f(g_k_x), "g_mem": f(g_mem), "g_mlp": f(g_mlp), "oh1": oh1, "bd": bdm, "ident": ident, "aident": np.ascontiguousarray(ident[::-1]),
    }
    in_maps = []
    for c in range(NCORES):
        b, hf = c // 2, c % 2
        m = dict(shared)
        m["xp"] = f(x_prompt[b, hf * TP:(hf + 1) * TP])
        m["xh"] = f(x_prompt[b, TP - 128:TP]) if hf == 1 else np.zeros((128, D), np.float32)
        m["xs"] = f(x_sample[c * 16:(c + 1) * 16].reshape(TS, D))
        m["xm"] = f(mem_prompt[b])
        m["cconv"] = f(cache_conv[0, c * 16:(c + 1) * 16].reshape(32, 512))
        m["ck"] = f(cache_swa_k[0, c * 16:(c + 1) * 16].reshape(16, 128, 256))
        m["cv"] = f(cache_swa_v[0, c * 16:(c + 1) * 16].reshape(16, 128, 256))
        m["cmk"] = f(cache_mem_k[0, c * 16:(c + 1) * 16].reshape(16, 256, 512))
        m["cmv"] = f(cache_mem_v[0, c * 16:(c + 1) * 16].reshape(16, 256, 512))
        m["hflag"] = np.full((128, 1), float(hf), np.float32)
        in_maps.append(m)

    res = run_bass_kernel_spmd(nc, in_maps, core_ids=list(range(NCORES)))
    R = res.results
    y_prompt = np.stack([np.concatenate([R[2 * b]["yp"], R[2 * b + 1]["yp"]], axis=0) for b in range(4)])
    y_sample = np.concatenate([R[c]["ys"].reshape(16, 8, D) for c in range(NCORES)], axis=0)
    p_conv = np.stack([R[2 * b + 1]["o_pconv"] for b in range(4)])[None]
    p_k = np.stack([R[2 * b + 1]["o_pk"].reshape(128, 2, 128) for b in range(4)])[None]
    p_v = np.stack([R[2 * b + 1]["o_pv"].reshape(128, 2, 128) for b in range(4)])[None]
    p_mk = np.stack([R[2 * b]["o_mk"].reshape(256, 4, 128) for b in range(4)])[None]
    p_mv = np.stack([R[2 * b]["o_mv"].reshape(256, 4, 128) for b in range(4)])[None]
    s_conv = np.concatenate([R[c]["o_sconv"] for c in range(NCORES)], axis=0)[None]
    s_k = np.concatenate([R[c]["o_sk"].reshape(16, 128, 2, 128) for c in range(NCORES)], axis=0)[None]
    s_v = np.concatenate([R[c]["o_sv"].reshape(16, 128, 2, 128) for c in range(NCORES)], axis=0)[None]
    outs = (y_prompt, y_sample, p_conv, p_k, p_v, p_mk, p_mv, s_conv, s_k, s_v)
    return tuple(np.ascontiguousarray(o, dtype=np.float32) for o in outs)
```
